# Optimizing a Trainium2 kernel written in Bass

```python
import math
import jax, jax.numpy as jnp
from jax import lax
import numpy as np

D_MODEL = 1024
BATCH = 8
SEQ = 2048
DEPTH = 1
DEC_BATCH = 32
DEC_SEQ = 4
PAST_LEN = 16384
PAGE_SIZE = 128

GLA_HEADS = 4
GLA_DK = D_MODEL // 2
GLA_DV = D_MODEL
GLA_HDK = GLA_DK // GLA_HEADS
GLA_HDV = GLA_DV // GLA_HEADS
GLA_GATE_RANK = 16
GLA_TAU = 16.0
GLA_CHUNK = 64
DIL_WINDOWS = (128, 512, 2048)
DIL_RATES = (1, 4, 16)
DIL_GROUPS = 3
DIL_HEADS = 4
DIL_HD = 64
DIL_WIDTH = DIL_GROUPS * DIL_HEADS * DIL_HD
DIL_OUT = DIL_HEADS * DIL_HD
Q_BLOCK = 128
ALIBI_MAX = 8.0
D_FF = -(-8 * D_MODEL // (3 * 256)) * 256
IN_SPLITS = (GLA_DK, GLA_DK, GLA_DV, GLA_DV, GLA_GATE_RANK, DIL_WIDTH, DIL_WIDTH, DIL_WIDTH, D_MODEL, D_MODEL)
N_IN = 2 * GLA_DK + 2 * GLA_DV + GLA_GATE_RANK + 3 * DIL_WIDTH + 2 * D_MODEL
RMS_EPS = 1e-6

kernel_name = 'gla_dilated_hybrid_step'


def rmsnorm(x, g):
    x32 = x.astype(jnp.float32)
    y = x32 * lax.rsqrt(jnp.mean(x32 * x32, axis=-1, keepdims=True) + RMS_EPS)
    return (y * g.astype(jnp.float32)).astype(x.dtype)


def split_cols(z):
    offs = np.cumsum(IN_SPLITS)[:-1].tolist()
    return jnp.split(z, offs, axis=-1)


def gla_chunked(q, k, v, log_a, s0):
    B, T, H, dk = q.shape
    dv = v.shape[-1]
    c = GLA_CHUNK if T % GLA_CHUNK == 0 else T
    n = T // c

    def to_chunks(a):
        return a.reshape(B, n, c, H, a.shape[-1]).swapaxes(0, 1)

    tril = jnp.tril(jnp.ones((c, c), dtype=bool))

    def step(S, inp):
        qc, kc, vc, ac = inp
        b = jnp.cumsum(ac, axis=1)
        o_inter = jnp.einsum('bthk,bhkv->bthv', qc * jnp.exp(b), S)
        diff = b[:, :, None] - b[:, None, :]
        decay = jnp.exp(jnp.where(tril[None, :, :, None, None], diff, -jnp.inf))
        scores = jnp.einsum('bthk,btshk,bshk->bhts', qc, decay, kc)
        o_intra = jnp.einsum('bhts,bshv->bthv', scores, vc)
        b_last = b[:, -1]
        S_new = jnp.exp(b_last)[..., None] * S + jnp.einsum(
            'bshk,bshv->bhkv', kc * jnp.exp(b_last[:, None] - b), vc)
        return S_new, o_inter + o_intra

    S_fin, o = lax.scan(step, s0, (to_chunks(q), to_chunks(k), to_chunks(v), to_chunks(log_a)))
    return o.swapaxes(0, 1).reshape(B, T, H, dv), S_fin


def alibi_slopes():
    n = DIL_GROUPS * DIL_HEADS
    s = jnp.exp2(-ALIBI_MAX * (jnp.arange(n, dtype=jnp.float32) + 1.0) / n)
    return s.reshape(DIL_GROUPS, DIL_HEADS)


def dilated_group(qb, qidx, k_ext, v_ext, rate, n_keys, slopes):
    steps = jnp.arange(n_keys, dtype=jnp.int32)
    idx = qidx[:, None] - rate * steps[None, :]
    valid = idx >= 0
    idx = jnp.maximum(idx, 0)
    kg = jnp.take(k_ext, idx, axis=1).astype(jnp.float32)
    vg = jnp.take(v_ext, idx, axis=1).astype(jnp.float32)
    dist = (rate * steps).astype(jnp.float32)
    s = jnp.einsum('bqhd,bqjhd->bhqj', qb, kg) - slopes[:, None, None] * dist
    s = jnp.where(valid[None, None], s, -jnp.inf)
    m = jnp.max(s, axis=-1, keepdims=True)
    p = jnp.exp(s - m)
    l = jnp.sum(p, axis=-1, keepdims=True)
    o = jnp.einsum('bhqj,bqjhd->bqhd', p / l, vg)
    log_den = jnp.swapaxes((m + jnp.log(l))[..., 0], 1, 2)
    return o, log_den


def dilated_block(qb, qidx_list, k_exts, v_exts, slopes):
    outs, dens = [], []
    for g in range(DIL_GROUPS):
        o, ld = dilated_group(qb[:, :, g], qidx_list[g], k_exts[g], v_exts[g],
                              DIL_RATES[g], DIL_WINDOWS[g] // DIL_RATES[g] + 1, slopes[g])
        outs.append(o)
        dens.append(ld)
    w = jax.nn.softmax(jnp.stack(dens), axis=0)
    return jnp.sum(w[..., None] * jnp.stack(outs), axis=0)


def dilated_mixer(q, k, v, k_bufs, v_bufs):
    B, T = q.shape[0], q.shape[1]
    q = q.astype(jnp.float32) * (DIL_HD ** -0.5)
    k_exts = [jnp.concatenate([k_bufs[g], k[:, :, g]], axis=1) for g in range(DIL_GROUPS)]
    v_exts = [jnp.concatenate([v_bufs[g], v[:, :, g]], axis=1) for g in range(DIL_GROUPS)]
    lens = [k_bufs[g].shape[1] for g in range(DIL_GROUPS)]
    slopes = alibi_slopes()

    def block(qb, start):
        tb = qb.shape[1]
        qidx = [lens[g] + start + jnp.arange(tb, dtype=jnp.int32) for g in range(DIL_GROUPS)]
        return dilated_block(qb, qidx, k_exts, v_exts, slopes)

    if T > Q_BLOCK and T % Q_BLOCK == 0:
        nb = T // Q_BLOCK
        qbs = q.reshape(B, nb, Q_BLOCK, DIL_GROUPS, DIL_HEADS, DIL_HD).swapaxes(0, 1)
        starts = jnp.arange(nb, dtype=jnp.int32) * Q_BLOCK
        o = lax.map(lambda a: block(a[0], a[1]), (qbs, starts))
        o = o.swapaxes(0, 1).reshape(B, T, DIL_HEADS, DIL_HD)
    else:
        o = block(q, jnp.int32(0))
    new_k = tuple(k_exts[g][:, -min(DIL_WINDOWS[g], k_exts[g].shape[1]):] for g in range(DIL_GROUPS))
    new_v = tuple(v_exts[g][:, -min(DIL_WINDOWS[g], v_exts[g].shape[1]):] for g in range(DIL_GROUPS))
    return o, new_k, new_v


def layer_forward(x, s_gla, k_bufs, v_bufs, norm1_g, w_in, gla_gate_w2, gla_gate_b, gla_norm_g,
                  proj_a, proj_b, w_out, norm2_g, w_ffn_gate, w_ffn_up, w_ffn_down):
    f32 = jnp.float32
    B, T, _ = x.shape
    h = rmsnorm(x, norm1_g)
    gq, gk, gv, gr, glr, dq, dk, dv, ga, gb = split_cols(h @ w_in)
    q_a = gq.reshape(B, T, GLA_HEADS, GLA_HDK).astype(f32) * (GLA_HDK ** -0.5)
    k_a = gk.reshape(B, T, GLA_HEADS, GLA_HDK).astype(f32)
    v_a = gv.reshape(B, T, GLA_HEADS, GLA_HDV).astype(f32)
    log_a = jax.nn.log_sigmoid((glr @ gla_gate_w2 + gla_gate_b).astype(f32)) / GLA_TAU
    log_a = log_a.reshape(B, T, GLA_HEADS, GLA_HDK)
    o_a, s_new = gla_chunked(q_a, k_a, v_a, log_a, s_gla.astype(f32))
    o_a = o_a * lax.rsqrt(jnp.mean(o_a * o_a, axis=-1, keepdims=True) + RMS_EPS) * gla_norm_g.astype(f32)
    o_a = (o_a.reshape(B, T, GLA_DV) * jax.nn.silu(gr.astype(f32))).astype(x.dtype)
    shp = (B, T, DIL_GROUPS, DIL_HEADS, DIL_HD)
    o_b, new_k, new_v = dilated_mixer(dq.reshape(shp), dk.reshape(shp), dv.reshape(shp), k_bufs, v_bufs)
    o_b = o_b.reshape(B, T, DIL_OUT).astype(x.dtype)
    merged = jax.nn.sigmoid(ga) * (o_a @ proj_a) + jax.nn.sigmoid(gb) * (o_b @ proj_b)
    x = x + merged @ w_out
    h2 = rmsnorm(x, norm2_g)
    x = x + (jax.nn.silu(h2 @ w_ffn_gate) * (h2 @ w_ffn_up)) @ w_ffn_down
    return x, s_new.astype(x.dtype), new_k, new_v


def run_trunk(x, s_gla, k_bufs, v_bufs, norm1_g, w_in, gla_gate_w2, gla_gate_b, gla_norm_g,
              proj_a, proj_b, w_out, norm2_g, w_ffn_gate, w_ffn_up, w_ffn_down, norm_f_g):
    gla_states, k_states, v_states = [], [], []
    for l in range(DEPTH):
        x, s, nk, nv = layer_forward(
            x, s_gla[l], tuple(b[l] for b in k_bufs), tuple(b[l] for b in v_bufs),
            norm1_g[l], w_in[l], gla_gate_w2[l], gla_gate_b[l], gla_norm_g[l],
            proj_a[l], proj_b[l], w_out[l], norm2_g[l], w_ffn_gate[l], w_ffn_up[l], w_ffn_down[l])
        gla_states.append(s)
        k_states.append(nk)
        v_states.append(nv)
    y = rmsnorm(x, norm_f_g)
    new_k = tuple(jnp.stack([ks[g] for ks in k_states]) for g in range(DIL_GROUPS))
    new_v = tuple(jnp.stack([vs[g] for vs in v_states]) for g in range(DIL_GROUPS))
    return y, jnp.stack(gla_states), new_k, new_v


def setup_inputs(seed: int = 0) -> dict:
    key = jax.random.key(seed)
    ks = jax.random.split(key, 24)

    def nrm(k, shape, scale):
        return jax.random.normal(k, shape, jnp.float32) * scale

    lens = [min(w, PAST_LEN) for w in DIL_WINDOWS]
    win_shape = lambda g: (DEPTH, DEC_BATCH, lens[g], DIL_HEADS, DIL_HD)
    return {
        'x_prompt': nrm(ks[0], (BATCH, SEQ, D_MODEL), 1.0),
        'x_sample': nrm(ks[1], (DEC_BATCH, DEC_SEQ, D_MODEL), 1.0),
        'state_gla': nrm(ks[2], (DEPTH, DEC_BATCH, GLA_HEADS, GLA_HDK, GLA_HDV), 0.5),
        'state_win0_k': nrm(ks[3], win_shape(0), 1.0),
        'state_win0_v': nrm(ks[4], win_shape(0), 1.0),
        'state_win1_k': nrm(ks[5], win_shape(1), 1.0),
        'state_win1_v': nrm(ks[6], win_shape(1), 1.0),
        'state_win2_k': nrm(ks[7], win_shape(2), 1.0),
        'state_win2_v': nrm(ks[8], win_shape(2), 1.0),
        'norm1_g': 1.0 + nrm(ks[9], (DEPTH, D_MODEL), 0.02),
        'w_in': nrm(ks[10], (DEPTH, D_MODEL, N_IN), D_MODEL ** -0.5),
        'gla_gate_w2': nrm(ks[11], (DEPTH, GLA_GATE_RANK, GLA_DK), GLA_GATE_RANK ** -0.5),
        'gla_gate_b': nrm(ks[12], (DEPTH, GLA_DK), 0.1),
        'gla_norm_g': 1.0 + nrm(ks[13], (DEPTH, GLA_HDV), 0.02),
        'proj_a': nrm(ks[14], (DEPTH, GLA_DV, D_MODEL), GLA_DV ** -0.5),
        'proj_b': nrm(ks[15], (DEPTH, DIL_OUT, D_MODEL), DIL_OUT ** -0.5),
        'w_out': nrm(ks[16], (DEPTH, D_MODEL, D_MODEL), D_MODEL ** -0.5),
        'norm2_g': 1.0 + nrm(ks[17], (DEPTH, D_MODEL), 0.02),
        'w_ffn_gate': nrm(ks[18], (DEPTH, D_MODEL, D_FF), D_MODEL ** -0.5),
        'w_ffn_up': nrm(ks[19], (DEPTH, D_MODEL, D_FF), D_MODEL ** -0.5),
        'w_ffn_down': nrm(ks[20], (DEPTH, D_FF, D_MODEL), D_FF ** -0.5),
        'norm_f_g': 1.0 + nrm(ks[21], (D_MODEL,), 0.02),
    }


def reference(x_prompt, x_sample, state_gla, state_win0_k, state_win0_v, state_win1_k, state_win1_v,
              state_win2_k, state_win2_v, norm1_g, w_in, gla_gate_w2, gla_gate_b, gla_norm_g,
              proj_a, proj_b, w_out, norm2_g, w_ffn_gate, w_ffn_up, w_ffn_down, norm_f_g):
    bp = x_prompt.shape[0]
    s0_prompt = jnp.zeros((DEPTH, bp, GLA_HEADS, GLA_HDK, GLA_HDV), x_prompt.dtype)
    empty = jnp.zeros((DEPTH, bp, 0, DIL_HEADS, DIL_HD), x_prompt.dtype)
    y_prompt, gla_p, (k0p, k1p, k2p), (v0p, v1p, v2p) = run_trunk(
        x_prompt, s0_prompt, (empty, empty, empty), (empty, empty, empty),
        norm1_g, w_in, gla_gate_w2, gla_gate_b, gla_norm_g, proj_a, proj_b, w_out,
        norm2_g, w_ffn_gate, w_ffn_up, w_ffn_down, norm_f_g)
    y_sample, gla_s, (k0s, k1s, k2s), (v0s, v1s, v2s) = run_trunk(
        x_sample, state_gla, (state_win0_k, state_win1_k, state_win2_k),
        (state_win0_v, state_win1_v, state_win2_v),
        norm1_g, w_in, gla_gate_w2, gla_gate_b, gla_norm_g, proj_a, proj_b, w_out,
        norm2_g, w_ffn_gate, w_ffn_up, w_ffn_down, norm_f_g)
    return (y_prompt, y_sample, gla_p, k0p, v0p, k1p, v1p, k2p, v2p,
            gla_s, k0s, v0s, k1s, v1s, k2s, v2s)
```

```python
import math
import numpy as np
import concourse.bass as bass
import concourse.mybir as mybir
from concourse.bass_utils import run_bass_kernel_spmd

F32 = mybir.dt.float32
BF16 = mybir.dt.bfloat16
I32 = mybir.dt.int32
AF = mybir.ActivationFunctionType
ALU = mybir.AluOpType
AX = mybir.AxisListType

NCORES = 8
D = 1024
T = 2048
NS = 16
NT = T + NS
NB = 17
DFF = 2816
NIN = 7440
EPS = 1e-6
WINS = (128, 512, 2048)
RATES = (1, 4, 16)
NEG = -30000.0
C_GQ, C_GK, C_GV, C_GR, C_GLR, C_DQ, C_DK, C_DV, C_GA, C_GB = 0, 512, 1024, 2048, 3072, 3088, 3856, 4624, 5392, 6416


def slope(g, h):
    return 2.0 ** (-8.0 * (g * 4 + h + 1.0) / 12.0)


class Inst:
    __slots__ = ("eng", "fn", "deps", "dma", "sig", "cnt", "sem", "semval", "ringwait", "idx", "epoch")


class Prog:
    ENGS = ("pe", "act", "dve", "pool", "sp")

    def __init__(self, nc):
        self.nc = nc
        self.q = {e: [] for e in self.ENGS}
        self.res = {}
        self.ndma = {e: 0 for e in self.ENGS}
        self.R = 8
        self.all = []
        self.bar_from = 0
        self.epoch = 0

    def op(self, eng, fn, r=(), w=(), dma=False):
        it = Inst()
        it.eng, it.fn, it.dma, it.sig, it.cnt = eng, fn, dma, False, 0
        it.sem = it.semval = it.ringwait = None
        it.epoch = self.epoch
        deps = set()
        isps = lambda k: isinstance(k, tuple) and k[0] in ("ps", "pst", "uz")
        w = list(w) + [k for k in r if isps(k) and k not in w]
        r = [k for k in r if not isps(k)]
        for k in r:
            st = self.res.setdefault(k, [None, {}])
            if st[0] is not None:
                deps.add(st[0])
            st[1][(eng if not dma else ("dma", len(self.all)))] = it
        for k in w:
            st = self.res.setdefault(k, [None, {}])
            if st[0] is not None:
                deps.add(st[0])
            deps.update(st[1].values())
            st[0] = it
            st[1] = {}
        deps.discard(it)
        it.deps = deps
        if dma:
            i = self.ndma[eng]
            self.ndma[eng] += 1
            it.sem = (eng, i % self.R)
            it.semval = 16 * (i // self.R + 1)
            it.ringwait = 16 * (i // self.R)
        it.idx = len(self.all)
        self.all.append(it)
        self.q[eng].append(it)
        return it

    def mm(self, out, lhsT, rhs, start, stop, r, w):
        return self.op("pe", lambda e: e.matmul(out, lhsT, rhs, start=start, stop=stop), r=r, w=w)

    def tr(self, out, in_, ident, r, w):
        return self.op("pe", lambda e: e.transpose(out, in_, ident), r=r, w=w)

    def act(self, out, in_, func, r, w, **kw):
        return self.op("act", lambda e: e.activation(out=out, in_=in_, func=func, **kw), r=r, w=w)

    def tt(self, eng, out, in0, in1, op, r, w):
        return self.op(eng, lambda e: e.tensor_tensor(out, in0, in1, op=op), r=r, w=w)

    def ts(self, eng, out, in0, s1, s2, op0, op1, r, w):
        return self.op(eng, lambda e: e.tensor_scalar(out, in0, s1, s2, op0=op0, op1=op1), r=r, w=w)

    def tsm(self, eng, out, in0, s1, r, w):
        return self.op(eng, lambda e: e.tensor_scalar_mul(out, in0, s1), r=r, w=w)

    def stt(self, eng, out, in0, scalar, in1, op0, op1, r, w):
        return self.op(eng, lambda e: e.scalar_tensor_tensor(out, in0, scalar, in1, op0=op0, op1=op1), r=r, w=w)

    def cp(self, eng, out, in_, r, w):
        if eng == "act":
            return self.op("act", lambda e: e.copy(out, in_), r=r, w=w)
        return self.op(eng, lambda e: e.tensor_copy(out, in_), r=r, w=w)

    def memset(self, eng, ap, val, w, r=()):
        return self.op(eng, lambda e: e.memset(ap, val), r=r, w=w)

    def asel(self, out, in_, pattern, cmp, fill, base, cm, r, w):
        return self.op("pool", lambda e: e.affine_select(out=out, in_=in_, pattern=pattern, compare_op=cmp, fill=fill,
                                                         base=base, channel_multiplier=cm), r=r, w=w)

    def dma(self, eng, out, in_, r=(), w=(), **kw):
        return self.op(eng, lambda e: e.dma_start(out=out, in_=in_, **kw), r=r, w=w, dma=True)

    def barrier(self):
        last = []
        for e in self.ENGS:
            real = [x for x in self.q[e] if x.fn is not None]
            if real:
                last.append(real[-1])
        last += [x for x in self.all[self.bar_from:] if x.dma]
        self.bar_from = len(self.all)
        self.epoch += 1
        for e in self.ENGS:
            it = self.op(e, None)
            it.deps = set(x for x in last)
        self.res = {}

    def emit(self):
        nc = self.nc
        for it in self.all:
            for d in it.deps:
                if not d.dma:
                    if d.eng == "pe" and it.eng == "pe" and not it.dma:
                        continue
                    d.sig = True
        for e in self.ENGS:
            c = {}
            for it in self.q[e]:
                if it.sig and not it.dma:
                    c[it.epoch] = c.get(it.epoch, 0) + 1
                it.cnt = c.get(it.epoch, 0)
        import contextlib
        with contextlib.ExitStack() as es:
            esem = {(e, ep): es.enter_context(nc.semaphore("s_%s_%d" % (e, ep))) for e in self.ENGS for ep in range(self.epoch + 1)}
            dsem = {}
            for e in self.ENGS:
                if self.ndma[e]:
                    for i in range(min(self.R, self.ndma[e])):
                        dsem[(e, i)] = es.enter_context(nc.semaphore("d_%s_%d" % (e, i)))
            block = es.enter_context(nc.Block())

            def run(ename, eng):
                known = {}
                lastdma = []
                for it in self.q[ename]:
                    waits = {}
                    for d in it.deps:
                        if d.dma:
                            key, val = ("d",) + d.sem, d.semval
                        else:
                            if d.eng == "pe" and ename == "pe" and not it.dma:
                                continue
                            if d.fn is None and d.eng == ename:
                                continue
                            key, val = ("e", d.eng, d.epoch), d.cnt
                            if val == 0:
                                continue
                        if known.get(key, 0) >= val:
                            continue
                        waits[key] = max(waits.get(key, 0), val)
                    if it.dma and it.ringwait:
                        key = ("d",) + it.sem
                        if known.get(key, 0) < it.ringwait:
                            waits[key] = max(waits.get(key, 0), it.ringwait)
                    for key, val in waits.items():
                        s = esem[(key[1], key[2])] if key[0] == "e" else dsem[(key[1], key[2])]
                        eng.wait_ge(s, val)
                        known[key] = val
                    if it.fn is None:
                        continue
                    bi = it.fn(eng)
                    if it.dma:
                        bi.then_inc(dsem[it.sem], 16)
                        lastdma.append(it)
                    elif it.sig:
                        bi.then_inc(esem[(ename, it.epoch)], 1)
                fin = {}
                for it in lastdma:
                    fin[it.sem] = max(fin.get(it.sem, 0), it.semval)
                for sm, val in fin.items():
                    eng.wait_ge(dsem[sm], val)

            @block.tensor
            def _(e):
                run("pe", e)

            @block.scalar
            def _(e):
                run("act", e)

            @block.vector
            def _(e):
                run("dve", e)

            @block.gpsimd
            def _(e):
                run("pool", e)

            @block.sync
            def _(e):
                run("sp", e)


def build_nc(stop_after=99, debug=False):
    nc = bass.Bass("TRN2", target_bir_lowering=False)
    P = Prog(nc)

    def din(name, shape):
        return nc.dram_tensor(name, list(shape), F32, kind="ExternalInput").ap()

    def dout(name, shape):
        return nc.dram_tensor(name, list(shape), F32, kind="ExternalOutput").ap()

    x_all = din("x_all", (NT, D))
    st_gla = din("st_gla", (4, 4, 128, 256))
    st_k = [din("st_k%d" % g, (4, WINS[g], 256)) for g in range(3)]
    st_v = [din("st_v%d" % g, (4, WINS[g], 256)) for g in range(3)]
    norm1_g = din("norm1_g", (1, D))
    w_in = din("w_in", (D, NIN))
    w2_d = din("w2", (16, 512))
    gb_d = din("gate_b", (1, 512))
    gng_d = din("gla_norm_g", (1, 256))
    proj_a = din("proj_a", (D, D))
    proj_b = din("proj_b", (256, D))
    w_out = din("w_out", (D, D))
    norm2_g = din("norm2_g", (1, D))
    wg_d = din("wg", (D, DFF))
    wu_d = din("wu", (D, DFF))
    wd_d = din("wd", (DFF, D))
    normf_g = din("normf_g", (1, D))

    y_all = dout("y_all", (NT, D))
    gla_p = dout("gla_p", (4, 128, 256))
    wk_p = [dout("wk_p%d" % g, (WINS[g], 256)) for g in range(3)]
    wv_p = [dout("wv_p%d" % g, (WINS[g], 256)) for g in range(3)]
    gla_s = dout("gla_s", (4, 4, 128, 256))
    wk_s = [dout("wk_s%d" % g, (4, WINS[g], 256)) for g in range(3)]
    wv_s = [dout("wv_s%d" % g, (4, WINS[g], 256)) for g in range(3)]

    ARENA = 206 * 1024
    arena = nc.alloc_sbuf_tensor("arena", [128, ARENA // 2], BF16).ap()

    def carve(off, free, dt, parts=128):
        n = 1
        for f in free:
            n *= f
        nbytes = n * (4 if dt in (F32, I32) else 2)
        assert off % 32 == 0, off
        assert off + nbytes <= ARENA, (off, nbytes)
        v = arena[0:parts, off // 2:(off + nbytes) // 2]
        if dt != BF16:
            v = v.bitcast(dt)
        if len(free) == 2:
            v = v.rearrange("p (a b) -> p a b", a=free[0])
        elif len(free) == 3:
            v = v.rearrange("p (a b c) -> p a b c", a=free[0], b=free[1])
        return v

    class Bump:
        def __init__(self, lo, hi):
            self.lo, self.hi, self.o = lo, hi, lo

        def __call__(self, free, dt, parts=128):
            n = 1
            for f in free:
                n *= f
            nb = n * (4 if dt in (F32, I32) else 2)
            nb = (nb + 31) // 32 * 32
            assert self.o + nb <= self.hi, ("bump overflow", self.lo, self.hi, self.o, nb)
            v = carve(self.o, free, dt, parts)
            self.o += nb
            return v

    KB = 1024
    RING0 = 0
    NSLOT = 6
    SLOT = 8 * KB
    MISC0 = 48 * KB
    MISC1 = 72 * KB
    MT0 = MISC1
    HT0 = MT0 + 33 * KB + 32
    OA0 = HT0 + 33 * KB + 32
    OB0 = OA0 + 33 * KB + 32
    TMP0 = OB0 + 17 * KB
    assert TMP0 < ARENA

    ring = [carve(RING0 + i * SLOT, (4096,), BF16) for i in range(NSLOT)]
    ring_i = [0]

    def slot():
        i = ring_i[0] % NSLOT
        ring_i[0] += 1
        return ring[i], ("ring", i)

    misc = Bump(MISC0, MISC1)
    ident = misc((128,), BF16)
    ones64 = misc((64,), BF16)
    triI = misc((128,), F32)
    triU = misc((128,), F32)
    g1T = misc((8,), F32)
    g2T = misc((8,), F32)
    gf_bc = misc((1024,), F32)
    gn_bc = misc((256,), F32)
    gbias_bc = misc((512,), F32)
    w2_sb = misc((512,), BF16, 16)
    stat = misc((64,), F32)
    xt = [misc((1024,), F32) for _ in range(2)]
    hb = misc((1024,), BF16)
    junk = misc((1024,), BF16)
    Dd = misc((256,), F32)
    Di = misc((256,), I32)

    hT = carve(HT0, (8, NT), BF16)
    mT = carve(MT0, (8, NT), BF16)
    oaT = carve(OA0, (8, NT), BF16)
    obT = [carve(OB0 + h * 4160, (NT,), BF16, 64) for h in range(4)]

    psb = [nc.alloc_psum_tensor("ps%d" % i, [128, 512], F32).ap() for i in range(6)]
    GEN = [0, 1, 2, 3]
    gen_i = [0]

    def psum():
        i = GEN[gen_i[0] % len(GEN)]
        gen_i[0] += 1
        return psb[i], ("ps", i)

    pstb = [nc.alloc_psum_tensor("pst%d" % i, [128, 1024], BF16).ap() for i in range(2)]
    evi = [0]
    GLA_PREFETCH = [True]

    def evac_eng():
        evi[0] += 1
        return "act" if evi[0] % 2 else "dve"

    def copy_op(eng, out, in_, r, w, scale=None):
        if scale is None:
            P.cp(eng, out, in_, r, w)
        elif eng == "act":
            P.op("act", lambda e: e.mul(out, in_, scale), r=r, w=w)
        else:
            P.tsm("dve", out, in_, scale, r, w)

    def tokn(tb):
        return 128 if tb < 16 else NS

    def consts():
        P.memset("pool", ident, 1.0, ["ident"])
        P.asel(ident, ident, [[-1, 128]], ALU.is_equal, 0.0, 0, 1, ["ident"], ["ident"])
        P.memset("pool", ones64, 1.0, ["ones64"])
        P.memset("pool", triI, 1.0, ["triI"])
        P.asel(triI, triI, [[1, 128]], ALU.is_ge, 0.0, 0, -1, ["triI"], ["triI"])
        P.memset("pool", triU, 1.0, ["triU"])
        P.asel(triU, triU, [[-1, 128]], ALU.is_gt, 0.0, 0, 1, ["triU"], ["triU"])
        P.memset("pool", stat, 0.0, ["stat"])
        P.op("pool", lambda e: e.iota(Di[:, 0:128], pattern=[[1, 128]], base=0, channel_multiplier=-1), w=["Di"])
        P.op("pool", lambda e: e.iota(Di[:, 128:256], pattern=[[1, 128]], base=128, channel_multiplier=-1), r=["Di"], w=["Di"])
        P.cp("pool", Dd, Di, ["Di"], ["Dd"])
        P.op("pool", lambda e: e.iota(Di, pattern=[[1, 256]], base=-64, channel_multiplier=0), r=["Dd"], w=["Di"])
        P.dma("sp", g1T, norm1_g.rearrange("o (k p) -> p (o k)", p=128), w=["g1T"], allow_slow_non_contiguous=True)
        P.dma("sp", g2T, norm2_g.rearrange("o (k p) -> p (o k)", p=128), w=["g2T"], allow_slow_non_contiguous=True)
        P.dma("sp", gf_bc, normf_g.partition_broadcast(128), w=["gf_bc"])
        P.dma("sp", gn_bc, gng_d.partition_broadcast(128), w=["gn_bc"])
        P.dma("sp", gbias_bc, gb_d.partition_broadcast(128), w=["gbias_bc"])
        P.dma("pool", w2_sb, w2_d, w=["w2"])

    hbufs = [hb, junk]

    def nt_front(src, src_keys, n, tb, statcol):
        sc = stat[0:n, statcol:statcol + 1]
        sk_ = ("stat", statcol)
        hbuf = hbufs[tb % 2]
        hk = ("hb", tb % 2)
        P.act(hbuf[0:n, :], src, AF.Square, src_keys, [hk, sk_], accum_out=sc)
        P.act(sc, sc, AF.Sqrt, [sk_], [sk_], bias=float(D * EPS))
        P.op("dve", lambda e, v_=sc: e.reciprocal(v_, v_), r=[sk_], w=[sk_])
        P.ts("dve", hbuf[0:n, :], src, sc, float(math.sqrt(D)), ALU.mult, ALU.mult, list(src_keys) + [sk_], [hk])

    def nt_back(n, tb, gT, gkey, dst, dst_keys):
        hbuf = hbufs[tb % 2]
        hk = ("hb", tb % 2)
        for half in range(2):
            for j in range(4):
                kc = half * 4 + j
                P.tr(pstb[half][:, j * 128:j * 128 + n], hbuf[0:n, kc * 128:(kc + 1) * 128], ident[0:n, 0:n], [hk, "ident"], [("pst", half)])
            for j in range(4):
                kc = half * 4 + j
                o_ = dst[:, kc, tb * 128:tb * 128 + n]
                i_ = pstb[half][:, j * 128:j * 128 + n]
                g_ = gT[:, kc:kc + 1]
                if half == 0:
                    P.op("act", lambda e, o_=o_, i_=i_, g_=g_: e.mul(o_, i_, g_), r=[("pst", half), gkey], w=dst_keys)
                else:
                    P.tsm("dve", o_, i_, g_, [("pst", half), gkey], dst_keys)

    def phase1():
        for tb in range(NB + 1):
            if tb < NB:
                n = tokn(tb)
                s = tb % 2
                P.dma("sp", xt[s][0:n, :], x_all[tb * 128:tb * 128 + n, :], w=[("xt", s)])
                nt_front(xt[s][0:n, :], [("xt", s)], n, tb, tb)
            if tb >= 1:
                nt_back(tokn(tb - 1), tb - 1, g1T, "g1T", hT, [("hT", tb - 1)])

    HT_ALL = [("hT", tb) for tb in range(NB)]
    NBLK = [(0, 512), (512, 512), (1024, 512), (1536, 512), (2048, NS)]

    def hkeys(c0, n):
        return [("hT", tb) for tb in range(c0 // 128, (c0 + n + 127) // 128)]

    def load_w(dst3, wd, c0, ncols, wkey):
        P.dma("pool", dst3, wd[:, c0:c0 + ncols].rearrange("(k p) n -> p k n", p=128), w=[wkey])

    def proj_fm(wv, wkey, dst, dstkey, scale=None, rows=128):
        for (c0, n) in NBLK:
            ps, pk = psum()
            for kc in range(8):
                P.mm(ps[0:rows, 0:n], wv[:, kc, :], hT[:, kc, c0:c0 + n], kc == 0, kc == 7, [wkey] + hkeys(c0, n), [pk])
            copy_op(evac_eng(), dst[:, c0:c0 + n], ps[0:rows, 0:n], [pk], [dstkey], scale=scale)

    def attention():
        A = Bump(MT0, MT0 + 33 * KB)
        qT = [A((NT,), BF16) for _ in range(3)]
        kT = [A((NT,), BF16) for _ in range(3)]
        stage = [A((256,), F32) for _ in range(2)]
        sbs = [A((256,), F32) for _ in range(4)]
        pp = [A((256,), BF16) for _ in range(4)]
        B2 = Bump(OA0, OA0 + 33 * KB)
        vcls = [B2((16, 128), BF16) for _ in range(3)]
        Uacc = B2((NT,), F32, 64)
        Zacc = B2((NT,), F32, 64)
        C3 = Bump(TMP0, ARENA)
        bias = [[C3((256,), F32) for h in range(2)] for g in range(3)]
        bp = [[C3((1,), F32) for h in range(2)] for g in range(3)]
        kgrp = [B2((4, 128), BF16) for _ in range(3)]
        vgrp = [C3((4, 128), BF16) for _ in range(3)]
        kTb = [C3((128,), BF16) for _ in range(3)]
        sbias0 = [C3((4,), F32) for _ in range(2)]
        sbiasg = C3((8,), F32)
        nbias = [[C3((4,), F32, 4) for h in range(2)] for g in range(3)]
        stage_s = [C3((256,), F32, 4) for _ in range(2)]
        vnew_t = {(g, sb): C3((128,), BF16, 4) for g in range(3) for sb in range(4)}
        pnew_t = [C3((4,), BF16, 4) for _ in range(8)]
        sbs_s = [C3((4,), F32) for _ in range(3)]
        pp_s2 = [[C3((4,), BF16) for _ in range(2)] for _ in range(3)]
        UaccS = [C3((16,), F32, 64) for _ in range(2)]
        ZaccS = [C3((16,), F32, 64) for _ in range(2)]
        UZ = [psb[4], psb[5]]
        cnt = {"stage": 0, "sbs": 0, "uz": 0, "kb": 0, "ss": 0, "pn": 0, "sbss": 0}

        def cls_cols(g, B):
            r = RATES[g]
            per = 16 // r
            c = B // per
            m0 = (B % per) * 128
            return c + r * m0, r

        def csl(st, rr, n=128):
            return slice(st, st + rr * (n - 1) + 1, rr)

        for hp in range(2):
            for g in range(3):
                for j in range(2):
                    h = 2 * hp + j
                    sl = slope(g, h) * RATES[g]
                    bt = bias[g][j]
                    bk = ("bias", g, j)
                    P.act(bt, Di, AF.Exp, ["Di"], [bk], scale=-sl)
                    P.asel(bt[:, 0:128], bt[:, 0:128], [[1, 128]], ALU.is_ge, 0.0, 0, -1, [bk], [bk])
                    P.asel(bt[:, 128:256], bt[:, 128:256], [[-1, 128]], ALU.is_ge, 0.0, 0, 1, [bk], [bk])
                    P.ts("pool", bp[g][j], Dd[:, 0:1], -sl, -64.0 * sl, ALU.mult, ALU.add, ["Dd"], [("bp", g, j)])
                    if g == 0:
                        sb0 = sbias0[j]
                        P.tsm("pool", sb0, Dd[:, 128:132], -sl, ["Dd"], [("sbias0", j)])
                        P.asel(sb0, sb0, [[-1, 4]], ALU.is_ge, NEG, 0, 1, [("sbias0", j)], [("sbias0", j)])
                    else:
                        col = (g - 1) * 2 + j
                        P.tsm("pool", sbiasg[:, col:col + 1], Dd[:, 128:129], -sl, ["Dd"], [("sbiasg", col)])
                    nb_ = nbias[g][j]
                    nk = ("nbias", g, j)
                    if g == 0:
                        P.tsm("pool", nb_, Dd[0:4, 0:4], -sl, ["Dd"], [nk])
                        P.asel(nb_, nb_, [[1, 4]], ALU.is_ge, NEG, 0, -1, [nk], [nk])
                    else:
                        P.memset("pool", nb_, 0.0, [nk])
                        P.asel(nb_, nb_, [[1, 4]], ALU.is_equal, NEG, 0, -1, [nk], [nk])
            for g in range(3):
                sl_ap, sk = slot()
                wv = sl_ap[:, 0:8 * 384].rearrange("p (k n) -> p k n", k=8)
                for i, cbase in enumerate((C_DQ, C_DK, C_DV)):
                    load_w(wv[:, :, i * 128:(i + 1) * 128], w_in, cbase + g * 256 + hp * 128, 128, sk)
                proj_fm(wv[:, :, 0:128], sk, qT[g], ("qT", g), scale=0.125)
                proj_fm(wv[:, :, 128:256], sk, kT[g], ("kT", g))
                r = RATES[g]
                per = 16 // r
                for B in range(16):
                    st, rr = cls_cols(g, B)
                    ps, pk = psum()
                    for kc in range(8):
                        P.mm(ps[:, 0:256], hT[:, kc, csl(st, rr)], wv[:, kc, 128:384], kc == 0, kc == 7, [sk] + HT_ALL[0:16], [pk])
                    P.cp("dve", vcls[g][:, B, :], ps[:, 128:256], [pk], [("vcls", g, B)])
                    if g == 0:
                        need = (B == 15)
                    elif g == 1:
                        need = (B % per) == per - 1
                    else:
                        need = True
                    if need:
                        si = cnt["stage"] % 2
                        cnt["stage"] += 1
                        P.cp("act", stage[si], ps[:, 0:256], [pk], [("stage", si)])
                        row0 = st - (T - WINS[g])
                        for (dst, c0) in ((wk_p[g], 0), (wv_p[g], 128)):
                            P.dma("sp", dst[csl(row0, rr), hp * 128:(hp + 1) * 128], stage[si][:, c0:c0 + 128], r=[("stage", si)])
                for sb in range(4):
                    ps, pk = psum()
                    for kc in range(8):
                        P.mm(ps[0:4, 0:256], hT[:, kc, T + 4 * sb:T + 4 * sb + 4], wv[:, kc, 128:384], kc == 0, kc == 7, [sk, ("hT", 16)], [pk])
                    si = cnt["ss"] % 2
                    cnt["ss"] += 1
                    P.cp("act", stage_s[si], ps[0:4, 0:256], [pk], [("stage_s", si)])
                    for (dst, c0) in ((wk_s[g], 0), (wv_s[g], 128)):
                        P.dma("sp", dst[sb, WINS[g] - 4:WINS[g], hp * 128:(hp + 1) * 128], stage_s[si][:, c0:c0 + 128], r=[("stage_s", si)])
                    P.cp("dve", vnew_t[(g, sb)], ps[0:4, 128:256], [pk], [("vnew", g, sb)])

            if hp == 1 and GLA_PREFETCH[0]:
                gla_prefetch()
            for j in range(2):
                h = 2 * hp + j
                pb0 = 64 * j
                blocks = [(g, B) for g in range(3) for B in range(16)]
                SK = 2
                uzsel = {}

                def afront(idx):
                    g, B = blocks[idx]
                    r = RATES[g]
                    per = 16 // r
                    st, rr = cls_cols(g, B)
                    hasprev = (B % per) != 0
                    Wd_ = 256 if hasprev else 128
                    qcols = qT[g][pb0:pb0 + 64, csl(st, rr)]
                    ps, pk = psum()
                    P.mm(ps[:, 0:128], kT[g][pb0:pb0 + 64, csl(st, rr)], qcols, True, True, [("qT", g), ("kT", g)], [pk])
                    if hasprev:
                        stp, _ = cls_cols(g, B - 1)
                        P.mm(ps[:, 128:256], kT[g][pb0:pb0 + 64, csl(stp, rr)], qcols, True, True, [("qT", g), ("kT", g)], [pk])
                    si = idx % 4
                    P.act(sbs[si][:, 0:Wd_], ps[:, 0:Wd_], AF.Exp, [pk, ("bp", g, j)], [("sbs", si)], bias=bp[g][j])
                    P.tt("dve" if idx % 2 == 0 else "pool", pp[si][:, 0:Wd_], sbs[si][:, 0:Wd_], bias[g][j][:, 0:Wd_], ALU.mult,
                         [("sbs", si), ("bias", g, j)], [("pp", si)])

                def aback(idx):
                    g, B = blocks[idx]
                    r = RATES[g]
                    per = 16 // r
                    hasprev = (B % per) != 0
                    si = idx % 4
                    i = B % 2
                    if i == 0:
                        uzsel[idx // 2] = cnt["uz"] % 2
                        cnt["uz"] += 1
                    ui = uzsel[idx // 2]
                    uz = UZ[ui]
                    uzk = ("uz", ui)
                    for isU, c0 in ((True, 0), (False, 256)):
                        lhs = vcls[g][:, B, j * 64:(j + 1) * 64] if isU else ones64
                        lk = [("vcls", g, B)] if isU else ["ones64"]
                        oap = uz[0:64, c0 + i * 128:c0 + (i + 1) * 128]
                        P.mm(oap, lhs, pp[si][:, 0:128], True, not hasprev, lk + [("pp", si)], [uzk])
                        if hasprev:
                            lhs2 = vcls[g][:, B - 1, j * 64:(j + 1) * 64] if isU else ones64
                            lk2 = [("vcls", g, B - 1)] if isU else ["ones64"]
                            P.mm(oap, lhs2, pp[si][:, 128:256], False, True, lk2 + [("pp", si)], [uzk])
                    if i == 1:
                        B0 = B - 1
                        st0, rr = cls_cols(g, B0)
                        for (acc, akey, c0) in ((Uacc, "Uacc", 0), (Zacc, "Zacc", 256)):
                            if r == 16:
                                dv = acc[:, 0:T].rearrange("p (i c) -> p c i", c=16)[:, B0:B0 + 2, :]
                                src = uz[0:64, c0:c0 + 256].rearrange("p (c i) -> p c i", c=2)
                            else:
                                dv = acc[:, csl(st0, rr, 256)]
                                src = uz[0:64, c0:c0 + 256]
                            if g == 0:
                                P.cp("act", dv, src, [uzk], [akey])
                            else:
                                P.tt("dve", dv, dv, src, ALU.add, [uzk, akey], [akey])

                for idx in range(len(blocks) + SK):
                    if idx < len(blocks):
                        afront(idx)
                    if idx - SK >= 0:
                        aback(idx - SK)

                P.op("dve", lambda e: e.reciprocal(Zacc[:, 0:T], Zacc[:, 0:T]), r=["Zacc"], w=["Zacc"])
                P.tt("dve", obT[h][:, 0:T], Uacc[:, 0:T], Zacc[:, 0:T], ALU.mult, ["Uacc", "Zacc"], [("obT", h)])

            sgroups = [(sb, g) for sb in range(4) for g in range(3)]
            stiles = [(sb, g, tq) for (sb, g) in sgroups for tq in range(1 if g == 0 else 4)]
            pn_of = {}

            def issue_grp(n):
                sb, g = sgroups[n]
                gi = n % 3
                r = RATES[g]
                cs = slice(hp * 128, (hp + 1) * 128)
                if g == 0:
                    P.dma("pool", kgrp[gi][:, 0, :], st_k[0][sb, :, cs], w=[("kgrp", gi)])
                    P.dma("pool", vgrp[gi][:, 0, :], st_v[0][sb, :, cs], w=[("vgrp", gi)])
                else:
                    P.dma("pool", kgrp[gi], st_k[g][sb].rearrange("(i t) c -> i t c", t=r)[:, 0:4, cs], w=[("kgrp", gi)])
                    P.dma("pool", vgrp[gi], st_v[g][sb].rearrange("(i t) c -> i t c", t=r)[:, 0:4, cs], w=[("vgrp", gi)])

            def sfa(ti):
                sb, g, tq = stiles[ti]
                n = sgroups.index((sb, g))
                gi = n % 3
                ki = ti % 3
                pb_ = ti % 2
                P.tr(pstb[pb_][:, 0:128], kgrp[gi][:, tq, :], ident, [("kgrp", gi), "ident"], [("pst", pb_)])
                P.cp("dve", kTb[ki], pstb[pb_][:, 0:128], [("pst", pb_)], [("kTb", ki)])

            def sfb(ti):
                sb, g, tq = stiles[ti]
                n = sgroups.index((sb, g))
                q0 = T + 4 * sb
                ki = ti % 3
                if tq == 0:
                    if n + 2 < len(sgroups):
                        issue_grp(n + 2)
                    for j in range(2):
                        pb0 = 64 * j
                        ps, pk = psum()
                        P.mm(ps[0:4, 0:4], kT[g][pb0:pb0 + 64, q0:q0 + 4], qT[g][pb0:pb0 + 64, q0:q0 + 4], True, True, [("qT", g), ("kT", g)], [pk])
                        sn = cnt["sbss"] % 3
                        cnt["sbss"] += 1
                        P.tt("dve", sbs_s[sn][0:4, :], ps[0:4, 0:4], nbias[g][j], ALU.add, [pk, ("nbias", g, j)], [("sbs_s", sn)])
                        pni = cnt["pn"] % 8
                        cnt["pn"] += 1
                        P.act(pnew_t[pni], sbs_s[sn][0:4, :], AF.Exp, [("sbs_s", sn)], [("pnew", pni)])
                        pn_of[(sb, g, j)] = pni
                nq = 4 if g == 0 else 1
                t0 = 0 if g == 0 else tq
                for j in range(2):
                    pb0 = 64 * j
                    ps, pk = psum()
                    P.mm(ps[:, 0:nq], kTb[ki][pb0:pb0 + 64, :], qT[g][pb0:pb0 + 64, q0 + t0:q0 + t0 + nq], True, True, [("kTb", ki), ("qT", g)], [pk])
                    ppk = ("pp_s", ki, j)
                    if g == 0:
                        sn = cnt["sbss"] % 3
                        cnt["sbss"] += 1
                        P.tt("dve", sbs_s[sn], ps[:, 0:4], sbias0[j], ALU.add, [pk, ("sbias0", j)], [("sbs_s", sn)])
                        P.act(pp_s2[ki][j], sbs_s[sn], AF.Exp, [("sbs_s", sn)], [ppk])
                    else:
                        col = (g - 1) * 2 + j
                        P.act(pp_s2[ki][j][:, 0:1], ps[:, 0:1], AF.Exp, [pk, ("sbiasg", col)], [ppk], bias=sbiasg[:, col:col + 1])

            def sback(ti):
                sb, g, tq = stiles[ti]
                n = sgroups.index((sb, g))
                gi = n % 3
                ki = ti % 3
                nq = 4 if g == 0 else 1
                t0 = 0 if g == 0 else tq
                vn = vnew_t[(g, sb)]
                for j in range(2):
                    uz = UZ[j]
                    uzk = ("uz", j)
                    pni = pn_of[(sb, g, j)]
                    pnew = pnew_t[pni]
                    pnk = ("pnew", pni)
                    ppk = ("pp_s", ki, j)
                    for isU, c0 in ((True, 0), (False, 256)):
                        lhs = vgrp[gi][:, tq, j * 64:(j + 1) * 64] if isU else ones64
                        lk = [("vgrp", gi)] if isU else ["ones64"]
                        cc = c0 + g * 4 + t0
                        P.mm(uz[0:64, cc:cc + nq], lhs, pp_s2[ki][j][:, 0:nq], True, False, lk + [ppk], [uzk])
                        lhs2 = vn[:, j * 64:(j + 1) * 64] if isU else ones64[0:4, :]
                        lk2 = [("vnew", g, sb)] if isU else ["ones64"]
                        P.mm(uz[0:64, cc:cc + nq], lhs2, pnew[:, t0:t0 + nq], False, True, lk2 + [pnk], [uzk])
                if ti + 1 == len(stiles) or stiles[ti + 1][0] != sb:
                    for j in range(2):
                        uz = UZ[j]
                        uzk = ("uz", j)
                        for (acc, akey, c0) in ((UaccS[j], ("UaccS", j), 0), (ZaccS[j], ("ZaccS", j), 256)):
                            dv = acc[:, 4 * sb:4 * sb + 4]
                            P.cp("dve", dv, uz[0:64, c0:c0 + 4], [uzk], [akey])
                            for g2 in (1, 2):
                                P.tt("dve", dv, dv, uz[0:64, c0 + 4 * g2:c0 + 4 * g2 + 4], ALU.add, [uzk, akey], [akey])

            issue_grp(0)
            issue_grp(1)
            NTL = len(stiles)
            sfa(0)
            for ti in range(NTL + 1):
                if ti + 1 < NTL:
                    sfa(ti + 1)
                if ti >= 1:
                    sback(ti - 1)
                if ti < NTL:
                    sfb(ti)
            for j in range(2):
                h = 2 * hp + j
                P.op("dve", lambda e, z_=ZaccS[j]: e.reciprocal(z_, z_), r=[("ZaccS", j)], w=[("ZaccS", j)])
                P.tt("dve", obT[h][:, T:T + NS], UaccS[j], ZaccS[j], ALU.mult, [("UaccS", j), ("ZaccS", j)], [("obTs", h)])

        for g in range(3):
            for (src, dst) in ((st_k[g], wk_s[g]), (st_v[g], wv_s[g])):
                for sb in range(4):
                    P.dma("sp", dst[sb, 0:WINS[g] - 4, :], src[sb, 4:WINS[g], :])

    GLA_W = []

    def gla_prefetch():
        for i in range(6):
            s_, k_ = slot()
            v_ = s_[:, 0:4096].rearrange("p (k n) -> p k n", k=8)
            load_w(v_, w_in, i * 512, 512, k_)
            GLA_W.append((v_, k_))

    def gla():
        A = Bump(MT0, MT0 + 33 * KB)
        C3 = Bump(TMP0, ARENA)
        glrT = A((NT,), BF16, 16)
        tri16 = A((128,), F32)
        triU16 = A((128,), F32)
        mask4 = A((4, 128), BF16)
        ktok = [A((512,), BF16) for _ in range(2)]
        vtok = [A((1024,), BF16) for _ in range(2)]
        sgr = [A((1024,), BF16) for _ in range(2)]
        la = [A((512,), F32) for _ in range(2)]
        qTc = A((4, 128), BF16)
        kTc = A((4, 128), BF16)
        eb = A((4, 128), F32)
        enb = A((4, 128), F32)
        ebb = A((512,), F32)
        qe = A((4, 128), BF16)
        ke = A((4, 128), BF16)
        kd = A((512,), BF16)
        AT = A((4, 128), BF16)
        S = [C3((4, 256), F32) for _ in range(2)]
        Sbf = [C3((4, 256), BF16) for _ in range(2)]
        o1 = [C3((256,), F32) for _ in range(2)]
        oat = C3((1024,), BF16)
        jq = C3((256,), BF16)
        gst = C3((8,), F32)
        epsb = C3((1,), F32)
        P.memset("pool", epsb, float(256 * EPS), ["epsb"])
        GEN6 = [0, 1, 2, 3]
        gi = [0]

        def ps6():
            i = GEN6[gi[0] % 4]
            gi[0] += 1
            return psb[i], ("ps", i)

        P.tsm("pool", tri16, triI, -1.0 / 16.0, ["triI"], ["tri16"])
        P.tsm("pool", triU16, triU, -1.0 / 16.0, ["triU"], ["triU16"])
        for hh in range(4):
            P.cp("pool", mask4[:, hh, :], triI, ["triI"], ["mask4"])

        wsl = GLA_W
        wq, wk = wsl[0], wsl[1]
        wvv = [wsl[2], wsl[3]]
        wgr = [wsl[4], wsl[5]]
        s_, k_ = wsl[5]
        wglr = C3((8, 16), BF16)
        load_w(wglr, w_in, C_GLR, 16, "wglr")
        for (c0, n) in NBLK:
            ps, pk = ps6()
            for kc in range(8):
                P.mm(ps[0:16, 0:n], wglr[:, kc, :], hT[:, kc, c0:c0 + n], kc == 0, kc == 7, ["wglr"], [pk])
            P.cp("act", glrT[:, c0:c0 + n], ps[0:16, 0:n], [pk], ["glrT"])

        chunks = [("p", 0, c * 128, 128, c == 0, c == 15) for c in range(16)] + [("s", sb, T + 4 * sb, 4, True, True) for sb in range(4)]

        def f_a(ci):
            kind, sb, c0, C, first, last = chunks[ci]
            b = ci % 2
            ps, pk = ps6()
            for kc in range(8):
                P.mm(ps[0:C, :], hT[:, kc, c0:c0 + C], wk[0][:, kc, :], kc == 0, kc == 7, [wk[1]], [pk])
            P.cp("act", ktok[b][0:C, :], ps[0:C, :], [pk], [("ktok", b)])
            for hf in range(2):
                ps, pk = ps6()
                for kc in range(8):
                    P.mm(ps[0:C, :], hT[:, kc, c0:c0 + C], wvv[hf][0][:, kc, :], kc == 0, kc == 7, [wvv[hf][1]], [pk])
                P.cp("dve", vtok[b][0:C, hf * 512:(hf + 1) * 512], ps[0:C, :], [pk], [("vtok", b)])
        def f_b(ci):
            kind, sb, c0, C, first, last = chunks[ci]
            b = ci % 2
            for hf in range(2):
                ps, pk = ps6()
                for kc in range(8):
                    P.mm(ps[0:C, :], hT[:, kc, c0:c0 + C], wgr[hf][0][:, kc, :], kc == 0, kc == 7, [wgr[hf][1]], [pk])
                P.act(sgr[b][0:C, hf * 512:(hf + 1) * 512], ps[0:C, :], AF.Silu, [pk], [("sgr", b)])
            ps, pk = ps6()
            P.mm(ps[0:C, :], glrT[:, c0:c0 + C], w2_sb, True, True, ["glrT", "w2"], [pk])
            P.tt("dve", la[b][0:C, :], ps[0:C, :], gbias_bc[0:C, :], ALU.add, [pk, "gbias_bc"], [("la", b)])
            P.act(la[b][0:C, :], la[b][0:C, :], AF.Exp, [("la", b)], [("la", b)], scale=-1.0)
            P.act(la[b][0:C, :], la[b][0:C, :], AF.Ln, [("la", b)], [("la", b)], bias=1.0)

        def m_a(ci):
            kind, sb, c0, C, first, last = chunks[ci]
            b = ci % 2
            psq, pkq = ps6()
            for h in range(4):
                for kc in range(8):
                    P.mm(psq[:, h * 128:h * 128 + C], wq[0][:, kc, h * 128:(h + 1) * 128], hT[:, kc, c0:c0 + C], kc == 0, kc == 7, [wq[1]], [pkq])
            P.tsm("dve", qTc[:, :, 0:C], psq.rearrange("p (h t) -> p h t", h=4)[:, :, 0:C], 128.0 ** -0.5, [pkq], ["qTc"])
            psk, pkk = ps6()
            for h in range(4):
                for kc in range(8):
                    P.mm(psk[:, h * 128:h * 128 + C], wk[0][:, kc, h * 128:(h + 1) * 128], hT[:, kc, c0:c0 + C], kc == 0, kc == 7, [wk[1]], [pkk])
            P.cp("act", kTc[:, :, 0:C], psk.rearrange("p (h t) -> p h t", h=4)[:, :, 0:C], [pkk], ["kTc"])
        def m_b(ci):
            kind, sb, c0, C, first, last = chunks[ci]
            b = ci % 2
            psb1, pk1 = ps6()
            for h in range(4):
                P.mm(psb1[:, h * 128:h * 128 + C], la[b][0:C, h * 128:(h + 1) * 128], tri16[0:C, 0:C], True, True, [("la", b), "tri16"], [pk1])
            psb2, pk2 = ps6()
            P.mm(psb2[0:C, :], triU16[0:C, 0:C], la[b][0:C, :], True, True, [("la", b), "triU16"], [pk2])
            bv = psb1.rearrange("p (h t) -> p h t", h=4)[:, :, 0:C]
            P.act(eb[:, :, 0:C], bv, AF.Exp, [pk1], ["eb"])
            P.act(enb[:, :, 0:C], bv, AF.Exp, [pk1], ["enb"], scale=-1.0)
            P.act(ebb[0:C, :], psb2[0:C, :], AF.Exp, [pk2], ["ebb"])
            P.tt("dve", qe[:, :, 0:C], qTc[:, :, 0:C], eb[:, :, 0:C], ALU.mult, ["qTc", "eb"], ["qe"])
            P.tt("dve", ke[:, :, 0:C], kTc[:, :, 0:C], enb[:, :, 0:C], ALU.mult, ["kTc", "enb"], ["ke"])
            P.tt("dve", kd[0:C, :], ktok[b][0:C, :], ebb[0:C, :], ALU.mult, [("ktok", b), "ebb"], ["kd"])
        def m_c(ci):
            kind, sb, c0, C, first, last = chunks[ci]
            b = ci % 2
            psa, pka = ps6()
            for h in range(4):
                P.mm(psa[0:C, h * 128:h * 128 + C], ke[:, h, 0:C], qe[:, h, 0:C], True, True, ["ke", "qe"], [pka])
            P.tt("dve", AT[0:C, :, 0:C], psa.rearrange("p (h t) -> p h t", h=4)[0:C, :, 0:C], mask4[0:C, :, 0:C], ALU.mult, [pka, "mask4"], ["AT"])

        pso_of = {}

        def b_a(ci):
            kind, sb, c0, C, first, last = chunks[ci]
            b = ci % 2
            sidx = 0 if kind == "p" else 1
            St, Sb = S[sidx], Sbf[sidx]
            Sk, Sbk = ("S", sidx), ("Sbf", sidx)
            if first:
                if kind == "p":
                    P.memset("pool", St, 0.0, [Sk])
                    P.memset("pool", Sb, 0.0, [Sbk])
                else:
                    P.dma("sp", St, st_gla[sb].rearrange("h k v -> k h v"), w=[Sk])
                    P.cp("act", Sb, St, [Sk], [Sbk])
            pso = [(psb[4], ("ps", 4)), (psb[5], ("ps", 5))]
            pso_of[ci] = pso
            for h in range(4):
                o_ap = pso[h // 2][0][0:C, (h % 2) * 256:(h % 2) * 256 + 256]
                ok = pso[h // 2][1]
                P.mm(o_ap, AT[0:C, h, 0:C], vtok[b][0:C, h * 256:(h + 1) * 256], True, False, ["AT", ("vtok", b)], [ok])
                P.mm(o_ap, qe[:, h, 0:C], Sb[:, h, :], False, True, ["qe", Sbk], [ok])
            psd = [ps6(), ps6()]
            for h in range(4):
                d_ap = psd[h // 2][0][:, (h % 2) * 256:(h % 2) * 256 + 256]
                P.mm(d_ap, kd[0:C, h * 128:(h + 1) * 128], vtok[b][0:C, h * 256:(h + 1) * 256], True, True, ["kd", ("vtok", b)], [psd[h // 2][1]])
            for h in range(4):
                d_ap = psd[h // 2][0][:, (h % 2) * 256:(h % 2) * 256 + 256]
                P.stt("dve", St[:, h, :], St[:, h, :], eb[:, h, C - 1:C], d_ap, ALU.mult, ALU.add, [Sk, "eb", psd[h // 2][1]], [Sk])
            P.cp("act", Sb, St, [Sk], [Sbk])
        def b_b(ci):
            kind, sb, c0, C, first, last = chunks[ci]
            b = ci % 2
            pso = pso_of[ci]
            gs = (ci % 2) * 4
            gk = ("gst", ci % 2)
            P.memset("pool", gst[0:C, gs:gs + 4], 0.0, [gk])
            for h in range(4):
                o_ap = pso[h // 2][0][0:C, (h % 2) * 256:(h % 2) * 256 + 256]
                P.act(jq[0:C, :], o_ap, AF.Square, [pso[h // 2][1]], ["jq", gk], accum_out=gst[0:C, gs + h:gs + h + 1])
            gv = gst[0:C, gs:gs + 4]
            P.act(gv, gv, AF.Ln, [gk, "epsb"], [gk], bias=epsb[0:C, 0:1])
            P.act(gv, gv, AF.Exp, [gk], [gk], scale=-0.5)
            for h in range(4):
                o_ap = pso[h // 2][0][0:C, (h % 2) * 256:(h % 2) * 256 + 256]
                ob_ = o1[h % 2]
                P.stt("dve", ob_[0:C, :], o_ap, gst[0:C, gs + h:gs + h + 1], gn_bc[0:C, :], ALU.mult, ALU.mult,
                      [pso[h // 2][1], gk, "gn_bc"], [("o1", h % 2)])
                P.stt("dve", oat[0:C, h * 256:(h + 1) * 256], ob_[0:C, :], 16.0, sgr[b][0:C, h * 256:(h + 1) * 256], ALU.mult, ALU.mult,
                      [("o1", h % 2), ("sgr", b)], ["oat"])
        def b_c(ci):
            kind, sb, c0, C, first, last = chunks[ci]
            b = ci % 2
            sidx = 0 if kind == "p" else 1
            St = S[sidx]
            Sk = ("S", sidx)
            pb_ = ci % 2
            for jj in range(8):
                P.tr(pstb[pb_][:, jj * 128:jj * 128 + C], oat[0:C, jj * 128:(jj + 1) * 128], ident[0:C, 0:C], ["oat", "ident"], [("pst", pb_)])
            pv = pstb[pb_].rearrange("p (a b) -> p a b", a=8)[:, :, 0:C]
            P.cp("act", oaT[:, :, c0:c0 + C], pv, [("pst", pb_)], [("oaT", c0)])
            if last:
                dst = gla_p if kind == "p" else gla_s[sb]
                P.dma("sp", dst.rearrange("h k v -> k h v"), St, r=[Sk])

        n = len(chunks)
        f_a(0)
        f_b(0)
        m_a(0)
        for ci in range(n):
            m_b(ci)
            if ci >= 1:
                b_c(ci - 1)
            if ci + 1 < n:
                f_a(ci + 1)
            m_c(ci)
            if ci + 1 < n:
                f_b(ci + 1)
            b_a(ci)
            if ci + 1 < n:
                m_a(ci + 1)
            b_b(ci)
        b_c(n - 1)

    X1_0 = HT0
    x1 = carve(X1_0, (NB, 1024), F32)
    AT0 = (X1_0 + NB * 4096 + 63) // 32 * 32
    assert AT0 + 30 * KB <= ARENA

    def phase3a():
        GEN[:] = [0, 1, 2, 3, 4, 5]
        C3 = Bump(TMP0, ARENA)
        sga = [C3((512,), F32) for _ in range(2)]
        sgb = [C3((512,), F32) for _ in range(2)]
        spk = "pbv"
        pbv = C3((4, 1024), BF16, 64)
        for hh in range(4):
            P.dma("pool", pbv[:, hh, :], proj_b[hh * 64:(hh + 1) * 64, :], w=[spk])
        k = 0
        import os
        V3 = os.environ.get('V3', 'gpbm')
        for oc in range(int(os.environ.get('NOC', '8'))):
            sl_ap, sk = slot()
            wv = sl_ap[:, 0:8 * 384].rearrange("p (k n) -> p k n", k=8)
            load_w(wv[:, :, 0:128], w_in, C_GA + oc * 128, 128, sk)
            load_w(wv[:, :, 128:256], w_in, C_GB + oc * 128, 128, sk)
            load_w(wv[:, :, 256:384], proj_a, oc * 128, 128, sk)
            for (c0, n) in NBLK:
                b = k % 2
                k += 1
                mk = [("mT", tb) for tb in range(c0 // 128, (c0 + n + 127) // 128)]
                if 'g' in V3:
                    pga, kga = psum()
                    for kc in range(8):
                        P.mm(pga[:, 0:n], wv[:, kc, 0:128], hT[:, kc, c0:c0 + n], kc == 0, kc == 7, [sk], [kga])
                    P.act(sga[b][:, 0:n], pga[:, 0:n], AF.Sigmoid, [kga], [("sga", b)])
                    pgb, kgb = psum()
                    for kc in range(8):
                        P.mm(pgb[:, 0:n], wv[:, kc, 128:256], hT[:, kc, c0:c0 + n], kc == 0, kc == 7, [sk], [kgb])
                    P.act(sgb[b][:, 0:n], pgb[:, 0:n], AF.Sigmoid, [kgb], [("sgb", b)])
                if 'p' in V3:
                    ppa, kpa = psum()
                    for kc in range(8):
                        P.mm(ppa[:, 0:n], wv[:, kc, 256:384], oaT[:, kc, c0:c0 + n], kc == 0, kc == 7, [sk], [kpa])
                    P.tt("dve", sga[b][:, 0:n], sga[b][:, 0:n], ppa[:, 0:n], ALU.mult, [kpa, ("sga", b)], [("sga", b)])
                if 'b' in V3:
                    ppb, kpb = psum()
                    for hh in range(4):
                        P.mm(ppb[:, 0:n], pbv[:, hh, oc * 128:(oc + 1) * 128], obT[hh][:, c0:c0 + n], hh == 0, hh == 3, [spk], [kpb])
                    P.tt("dve", sgb[b][:, 0:n], sgb[b][:, 0:n], ppb[:, 0:n], ALU.mult, [kpb, ("sgb", b)], [("sgb", b)])
                if 'm' in V3:
                    P.tt("dve", mT[:, oc, c0:c0 + n], sga[b][:, 0:n], sgb[b][:, 0:n], ALU.add, [("sga", b), ("sgb", b)], mk)

    WO = []

    def p3b_prefetch():
        for hf in range(2):
            s_, k_ = slot()
            v = s_[:, 0:4096].rearrange("p (k n) -> p k n", k=8)
            load_w(v, w_out, hf * 512, 512, k_)
            WO.append((v, k_))

    def phase3b():
        wo = WO
        for tb in range(NB + 1):
            if tb < NB:
                n = tokn(tb)
                s = tb % 2
                P.dma("sp", xt[s][0:n, :], x_all[tb * 128:tb * 128 + n, :], w=[("xt", s)])
                for hf in range(2):
                    ps, pk = psum()
                    v, k_ = wo[hf]
                    for kc in range(8):
                        P.mm(ps[0:n, :], mT[:, kc, tb * 128:tb * 128 + n], v[:, kc, :], kc == 0, kc == 7, [k_, ("mT", tb)], [pk])
                    P.tt("dve", x1[0:n, tb, hf * 512:(hf + 1) * 512], ps[0:n, :], xt[s][0:n, hf * 512:(hf + 1) * 512], ALU.add,
                         [pk, ("xt", s)], [("x1", tb)])
                nt_front(x1[0:n, tb, :], [("x1", tb)], n, tb, 20 + tb)
            if tb >= 1:
                nt_back(tokn(tb - 1), tb - 1, g2T, "g2T", mT, [("mT", tb - 1)])

    P4W = {}

    def p4_load(grp):
        sl1, k1 = slot()
        wgu = sl1[:, 0:4096].rearrange("p (k n) -> p k n", k=8)
        load_w(wgu[:, :, 0:256], wg_d, grp * 256, 256, k1)
        load_w(wgu[:, :, 256:512], wu_d, grp * 256, 256, k1)
        sl2, k2 = slot()
        wdv = sl2[:, 0:2048].rearrange("p (c n) -> p c n", c=2)
        P.dma("pool", wdv, wd_d[grp * 256:(grp + 1) * 256, :].rearrange("(c p) n -> p c n", p=128), w=[k2])
        P4W[grp] = (wgu, k1, wdv, k2)

    def phase4():
        A = Bump(AT0, ARENA)
        aT = [A((2, NT), BF16) for _ in range(2)]
        sgt = [A((512,), F32) for _ in range(2)]
        yt = [A((1024,), F32) for _ in range(2)]
        h2T = mT
        k = 0
        for grp in range(11):
            ab = grp % 2
            if grp not in P4W:
                p4_load(grp)
            wgu, k1, wdv, k2 = P4W[grp]
            for c in range(2):
                for (c0, n) in NBLK:
                    b = k % 2
                    k += 1
                    pg, kg = psum()
                    for kc in range(8):
                        P.mm(pg[:, 0:n], wgu[:, kc, c * 128:(c + 1) * 128], h2T[:, kc, c0:c0 + n], kc == 0, kc == 7, [k1], [kg])
                    P.act(sgt[b][:, 0:n], pg[:, 0:n], AF.Silu, [kg], [("sgt", b)])
                    pu, ku = psum()
                    for kc in range(8):
                        P.mm(pu[:, 0:n], wgu[:, kc, 256 + c * 128:256 + (c + 1) * 128], h2T[:, kc, c0:c0 + n], kc == 0, kc == 7, [k1], [ku])
                    P.tt("dve", aT[ab][:, c, c0:c0 + n], sgt[b][:, 0:n], pu[:, 0:n], ALU.mult, [ku, ("sgt", b)], [("aT", ab)])
            for tb in range(NB):
                n = tokn(tb)
                for hf in range(2):
                    ps, pk = psum()
                    for c in range(2):
                        P.mm(ps[0:n, :], aT[ab][:, c, tb * 128:tb * 128 + n], wdv[:, c, hf * 512:(hf + 1) * 512], c == 0, c == 1, [k2, ("aT", ab)], [pk])
                    xv = x1[0:n, tb, hf * 512:(hf + 1) * 512]
                    P.tt("dve", xv, xv, ps[0:n, :], ALU.add, [pk, ("x1", tb)], [("x1", tb)])
        P.tsm("dve", gf_bc, gf_bc, float(math.sqrt(D)), ["gf_bc"], ["gf_bc"])
        for tb in range(NB):
            n = tokn(tb)
            b = tb % 2
            col = 40 + tb
            sc = stat[0:n, col:col + 1]
            sk_ = ("stat", col)
            P.act(junk[0:n, :], x1[0:n, tb, :], AF.Square, [("x1", tb)], ["junk", sk_], accum_out=sc)
            P.act(sc, sc, AF.Sqrt, [sk_], [sk_], bias=float(D * EPS))
            P.op("dve", lambda e, v_=sc: e.reciprocal(v_, v_), r=[sk_], w=[sk_])
            P.stt("dve", yt[b][0:n, :], x1[0:n, tb, :], sc, gf_bc[0:n, :], ALU.mult, ALU.mult, [("x1", tb), sk_, "gf_bc"], [("yt", b)])
            P.dma("sp", y_all[tb * 128:tb * 128 + n, :], yt[b][0:n, :], r=[("yt", b)])

    consts()
    import os
    SKIP = os.environ.get('SKIP', '')
    if stop_after >= 1 and '1' not in SKIP:
        phase1()
    if stop_after >= 2 and '2' not in SKIP:
        attention()
    if stop_after >= 3 and '3' not in SKIP:
        P.barrier()
        gla()
    if stop_after >= 4:
        P.barrier()
        import os
        phase3a()
        if os.environ.get('PH3', 'ab') == 'ab':
            p3b_prefetch()
            P.barrier()
            phase3b()
            if stop_after >= 5:
                p4_load(0)
                p4_load(1)
    if stop_after >= 5:
        P.barrier()
        phase4()
    if debug:
        dbg = {}
        P.barrier()
        for name, ap, shape in (("d_hT", hT, (128, 8 * NT)), ("d_oaT", oaT, (128, 8 * NT)), ("d_mT", mT, (128, 8 * NT))):
            d = nc.dram_tensor(name, [128, 8 * NT], BF16, kind="ExternalOutput").ap()
            P.op("sp", lambda e, d=d, ap=ap: e.dma_start(out=d.rearrange("p (a b) -> p a b", a=8), in_=ap), dma=True)
        for h in range(4):
            d = nc.dram_tensor("d_obT%d" % h, [64, NT], BF16, kind="ExternalOutput").ap()
            P.op("sp", lambda e, d=d, h=h: e.dma_start(out=d, in_=obT[h]), dma=True)
    P.emit()
    return nc


_NC_CACHE = {}


def make_in_maps(inp):
    f = lambda a: np.ascontiguousarray(np.asarray(a, dtype=np.float32))
    xp, xs = f(inp["x_prompt"]), f(inp["x_sample"])
    shared = {
        "norm1_g": f(inp["norm1_g"]).reshape(1, D), "w_in": f(inp["w_in"])[0], "w2": f(inp["gla_gate_w2"])[0],
        "gate_b": f(inp["gla_gate_b"]).reshape(1, 512), "gla_norm_g": f(inp["gla_norm_g"]).reshape(1, 256),
        "proj_a": f(inp["proj_a"])[0], "proj_b": f(inp["proj_b"])[0], "w_out": f(inp["w_out"])[0],
        "norm2_g": f(inp["norm2_g"]).reshape(1, D), "wg": f(inp["w_ffn_gate"])[0], "wu": f(inp["w_ffn_up"])[0],
        "wd": f(inp["w_ffn_down"])[0], "normf_g": f(inp["norm_f_g"]).reshape(1, D),
    }
    sg = f(inp["state_gla"])[0]
    sk = [f(inp["state_win0_k"])[0], f(inp["state_win1_k"])[0], f(inp["state_win2_k"])[0]]
    sv = [f(inp["state_win0_v"])[0], f(inp["state_win1_v"])[0], f(inp["state_win2_v"])[0]]
    maps = []
    for c in range(NCORES):
        m = dict(shared)
        m["x_all"] = np.ascontiguousarray(np.concatenate([xp[c], xs[4 * c:4 * c + 4].reshape(NS, D)], axis=0))
        m["st_gla"] = np.ascontiguousarray(sg[4 * c:4 * c + 4])
        for g in range(3):
            m["st_k%d" % g] = np.ascontiguousarray(sk[g][4 * c:4 * c + 4].reshape(4, WINS[g], 256))
            m["st_v%d" % g] = np.ascontiguousarray(sv[g][4 * c:4 * c + 4].reshape(4, WINS[g], 256))
        maps.append(m)
    return maps


def kernel(**inputs):
    if "nc" not in _NC_CACHE:
        _NC_CACHE["nc"] = build_nc()
    nc = _NC_CACHE["nc"]
    maps = make_in_maps(inputs)
    res = run_bass_kernel_spmd(nc, maps, core_ids=list(range(NCORES)))
    R = res.results
    y_all = np.stack([R[c]["y_all"] for c in range(NCORES)])
    y_prompt = np.ascontiguousarray(y_all[:, :T, :])
    y_sample = np.ascontiguousarray(y_all[:, T:, :].reshape(32, 4, D))
    gla_p = np.stack([R[c]["gla_p"] for c in range(NCORES)])[None]
    outs = [y_prompt, y_sample, gla_p]
    for g in range(3):
        outs.append(np.stack([R[c]["wk_p%d" % g] for c in range(NCORES)]).reshape(1, 8, WINS[g], 4, 64))
        outs.append(np.stack([R[c]["wv_p%d" % g] for c in range(NCORES)]).reshape(1, 8, WINS[g], 4, 64))
    outs.append(np.concatenate([R[c]["gla_s"] for c in range(NCORES)], axis=0)[None])
    for g in range(3):
        outs.append(np.concatenate([R[c]["wk_s%d" % g] for c in range(NCORES)], axis=0).reshape(1, 32, WINS[g], 4, 64))
        outs.append(np.concatenate([R[c]["wv_s%d" % g] for c in range(NCORES)], axis=0).reshape(1, 32, WINS[g], 4, 64))
    return tuple(np.ascontiguousarray(o.astype(np.float32)) for o in outs)
```

```python
import math
import numpy as np
import concourse.bass as bass
import concourse.mybir as mybir
from concourse.bass_utils import run_bass_kernel_spmd

F32 = mybir.dt.float32
BF16 = mybir.dt.bfloat16
I32 = mybir.dt.int32
AF = mybir.ActivationFunctionType
ALU = mybir.AluOpType
AX = mybir.AxisListType

NCORES = 8
D = 1024
T = 2048
NS = 16
NT = T + NS
NB = 17
DFF = 2816
NIN = 7440
EPS = 1e-6
WINS = (128, 512, 2048)
RATES = (1, 4, 16)
NEG = -30000.0
C_GQ, C_GK, C_GV, C_GR, C_GLR, C_DQ, C_DK, C_DV, C_GA, C_GB = 0, 512, 1024, 2048, 3072, 3088, 3856, 4624, 5392, 6416


def slope(g, h):
    return 2.0 ** (-8.0 * (g * 4 + h + 1.0) / 12.0)


class Inst:
    __slots__ = ("eng", "fn", "deps", "dma", "sig", "cnt", "sem", "semval", "ringwait", "idx", "epoch")


class Prog:
    ENGS = ("pe", "act", "dve", "pool", "sp")

    def __init__(self, nc):
        self.nc = nc
        self.q = {e: [] for e in self.ENGS}
        self.res = {}
        self.ndma = {e: 0 for e in self.ENGS}
        self.R = 8
        self.all = []
        self.bar_from = 0
        self.epoch = 0

    def op(self, eng, fn, r=(), w=(), dma=False):
        it = Inst()
        it.eng, it.fn, it.dma, it.sig, it.cnt = eng, fn, dma, False, 0
        it.sem = it.semval = it.ringwait = None
        it.epoch = self.epoch
        deps = set()
        isps = lambda k: isinstance(k, tuple) and k[0] in ("ps", "pst", "uz")
        w = list(w) + [k for k in r if isps(k) and k not in w]
        r = [k for k in r if not isps(k)]
        for k in r:
            st = self.res.setdefault(k, [None, {}])
            if st[0] is not None:
                deps.add(st[0])
            st[1][(eng if not dma else ("dma", len(self.all)))] = it
        for k in w:
            st = self.res.setdefault(k, [None, {}])
            if st[0] is not None:
                deps.add(st[0])
            deps.update(st[1].values())
            st[0] = it
            st[1] = {}
        deps.discard(it)
        it.deps = deps
        if dma:
            i = self.ndma[eng]
            self.ndma[eng] += 1
            it.sem = (eng, i % self.R)
            it.semval = 16 * (i // self.R + 1)
            it.ringwait = 16 * (i // self.R)
        it.idx = len(self.all)
        self.all.append(it)
        self.q[eng].append(it)
        return it

    def mm(self, out, lhsT, rhs, start, stop, r, w):
        return self.op("pe", lambda e: e.matmul(out, lhsT, rhs, start=start, stop=stop), r=r, w=w)

    def tr(self, out, in_, ident, r, w):
        return self.op("pe", lambda e: e.transpose(out, in_, ident), r=r, w=w)

    def act(self, out, in_, func, r, w, **kw):
        return self.op("act", lambda e: e.activation(out=out, in_=in_, func=func, **kw), r=r, w=w)

    def tt(self, eng, out, in0, in1, op, r, w):
        return self.op(eng, lambda e: e.tensor_tensor(out, in0, in1, op=op), r=r, w=w)

    def ts(self, eng, out, in0, s1, s2, op0, op1, r, w):
        return self.op(eng, lambda e: e.tensor_scalar(out, in0, s1, s2, op0=op0, op1=op1), r=r, w=w)

    def tsm(self, eng, out, in0, s1, r, w):
        return self.op(eng, lambda e: e.tensor_scalar_mul(out, in0, s1), r=r, w=w)

    def stt(self, eng, out, in0, scalar, in1, op0, op1, r, w):
        return self.op(eng, lambda e: e.scalar_tensor_tensor(out, in0, scalar, in1, op0=op0, op1=op1), r=r, w=w)

    def cp(self, eng, out, in_, r, w):
        if eng == "act":
            return self.op("act", lambda e: e.copy(out, in_), r=r, w=w)
        return self.op(eng, lambda e: e.tensor_copy(out, in_), r=r, w=w)

    def memset(self, eng, ap, val, w, r=()):
        return self.op(eng, lambda e: e.memset(ap, val), r=r, w=w)

    def asel(self, out, in_, pattern, cmp, fill, base, cm, r, w):
        return self.op("pool", lambda e: e.affine_select(out=out, in_=in_, pattern=pattern, compare_op=cmp, fill=fill,
                                                         base=base, channel_multiplier=cm), r=r, w=w)

    def dma(self, eng, out, in_, r=(), w=(), **kw):
        return self.op(eng, lambda e: e.dma_start(out=out, in_=in_, **kw), r=r, w=w, dma=True)

    def barrier(self):
        last = []
        for e in self.ENGS:
            real = [x for x in self.q[e] if x.fn is not None]
            if real:
                last.append(real[-1])
        last += [x for x in self.all[self.bar_from:] if x.dma]
        self.bar_from = len(self.all)
        self.epoch += 1
        for e in self.ENGS:
            it = self.op(e, None)
            it.deps = set(x for x in last)
        self.res = {}

    def emit(self):
        nc = self.nc
        for it in self.all:
            for d in it.deps:
                if not d.dma:
                    if d.eng == "pe" and it.eng == "pe" and not it.dma:
                        continue
                    d.sig = True
        for e in self.ENGS:
            c = {}
            for it in self.q[e]:
                if it.sig and not it.dma:
                    c[it.epoch] = c.get(it.epoch, 0) + 1
                it.cnt = c.get(it.epoch, 0)
        import contextlib
        with contextlib.ExitStack() as es:
            esem = {(e, ep): es.enter_context(nc.semaphore("s_%s_%d" % (e, ep))) for e in self.ENGS for ep in range(self.epoch + 1)}
            dsem = {}
            for e in self.ENGS:
                if self.ndma[e]:
                    for i in range(min(self.R, self.ndma[e])):
                        dsem[(e, i)] = es.enter_context(nc.semaphore("d_%s_%d" % (e, i)))
            block = es.enter_context(nc.Block())

            def run(ename, eng):
                known = {}
                lastdma = []
                for it in self.q[ename]:
                    waits = {}
                    for d in it.deps:
                        if d.dma:
                            key, val = ("d",) + d.sem, d.semval
                        else:
                            if d.eng == "pe" and ename == "pe" and not it.dma:
                                continue
                            if d.fn is None and d.eng == ename:
                                continue
                            key, val = ("e", d.eng, d.epoch), d.cnt
                            if val == 0:
                                continue
                        if known.get(key, 0) >= val:
                            continue
                        waits[key] = max(waits.get(key, 0), val)
                    if it.dma and it.ringwait:
                        key = ("d",) + it.sem
                        if known.get(key, 0) < it.ringwait:
                            waits[key] = max(waits.get(key, 0), it.ringwait)
                    for key, val in waits.items():
                        s = esem[(key[1], key[2])] if key[0] == "e" else dsem[(key[1], key[2])]
                        eng.wait_ge(s, val)
                        known[key] = val
                    if it.fn is None:
                        continue
                    bi = it.fn(eng)
                    if it.dma:
                        bi.then_inc(dsem[it.sem], 16)
                        lastdma.append(it)
                    elif it.sig:
                        bi.then_inc(esem[(ename, it.epoch)], 1)
                fin = {}
                for it in lastdma:
                    fin[it.sem] = max(fin.get(it.sem, 0), it.semval)
                for sm, val in fin.items():
                    eng.wait_ge(dsem[sm], val)

            @block.tensor
            def _(e):
                run("pe", e)

            @block.scalar
            def _(e):
                run("act", e)

            @block.vector
            def _(e):
                run("dve", e)

            @block.gpsimd
            def _(e):
                run("pool", e)

            @block.sync
            def _(e):
                run("sp", e)


def build_nc(stop_after=99, debug=False):
    nc = bass.Bass("TRN2", target_bir_lowering=False)
    P = Prog(nc)

    def din(name, shape):
        return nc.dram_tensor(name, list(shape), F32, kind="ExternalInput").ap()

    def dout(name, shape):
        return nc.dram_tensor(name, list(shape), F32, kind="ExternalOutput").ap()

    x_all = din("x_all", (NT, D))
    st_gla = din("st_gla", (4, 4, 128, 256))
    st_k = [din("st_k%d" % g, (4, WINS[g], 256)) for g in range(3)]
    st_v = [din("st_v%d" % g, (4, WINS[g], 256)) for g in range(3)]
    norm1_g = din("norm1_g", (1, D))
    w_in = din("w_in", (D, NIN))
    w2_d = din("w2", (16, 512))
    gb_d = din("gate_b", (1, 512))
    gng_d = din("gla_norm_g", (1, 256))
    proj_a = din("proj_a", (D, D))
    proj_b = din("proj_b", (256, D))
    w_out = din("w_out", (D, D))
    norm2_g = din("norm2_g", (1, D))
    wg_d = din("wg", (D, DFF))
    wu_d = din("wu", (D, DFF))
    wd_d = din("wd", (DFF, D))
    normf_g = din("normf_g", (1, D))

    y_all = dout("y_all", (NT, D))
    gla_p = dout("gla_p", (4, 128, 256))
    wk_p = [dout("wk_p%d" % g, (WINS[g], 256)) for g in range(3)]
    wv_p = [dout("wv_p%d" % g, (WINS[g], 256)) for g in range(3)]
    gla_s = dout("gla_s", (4, 4, 128, 256))
    wk_s = [dout("wk_s%d" % g, (4, WINS[g], 256)) for g in range(3)]
    wv_s = [dout("wv_s%d" % g, (4, WINS[g], 256)) for g in range(3)]

    ARENA = 206 * 1024
    arena = nc.alloc_sbuf_tensor("arena", [128, ARENA // 2], BF16).ap()

    def carve(off, free, dt, parts=128):
        n = 1
        for f in free:
            n *= f
        nbytes = n * (4 if dt in (F32, I32) else 2)
        assert off % 32 == 0, off
        assert off + nbytes <= ARENA, (off, nbytes)
        v = arena[0:parts, off // 2:(off + nbytes) // 2]
        if dt != BF16:
            v = v.bitcast(dt)
        if len(free) == 2:
            v = v.rearrange("p (a b) -> p a b", a=free[0])
        elif len(free) == 3:
            v = v.rearrange("p (a b c) -> p a b c", a=free[0], b=free[1])
        return v

    class Bump:
        def __init__(self, lo, hi):
            self.lo, self.hi, self.o = lo, hi, lo

        def __call__(self, free, dt, parts=128):
            n = 1
            for f in free:
                n *= f
            nb = n * (4 if dt in (F32, I32) else 2)
            nb = (nb + 31) // 32 * 32
            assert self.o + nb <= self.hi, ("bump overflow", self.lo, self.hi, self.o, nb)
            v = carve(self.o, free, dt, parts)
            self.o += nb
            return v

    KB = 1024
    RING0 = 0
    NSLOT = 6
    SLOT = 8 * KB
    MISC0 = 48 * KB
    MISC1 = 72 * KB
    MT0 = MISC1
    HT0 = MT0 + 33 * KB + 32
    OA0 = HT0 + 33 * KB + 32
    OB0 = OA0 + 33 * KB + 32
    TMP0 = OB0 + 17 * KB
    assert TMP0 < ARENA

    ring = [carve(RING0 + i * SLOT, (4096,), BF16) for i in range(NSLOT)]
    ring_i = [0]

    def slot():
        i = ring_i[0] % NSLOT
        ring_i[0] += 1
        return ring[i], ("ring", i)

    misc = Bump(MISC0, MISC1)
    ident = misc((128,), BF16)
    ones64 = misc((64,), BF16)
    triI = misc((128,), F32)
    triU = misc((128,), F32)
    g1T = misc((8,), F32)
    g2T = misc((8,), F32)
    gf_bc = misc((1024,), F32)
    gn_bc = misc((256,), F32)
    gbias_bc = misc((512,), F32)
    w2_sb = misc((512,), BF16, 16)
    stat = misc((64,), F32)
    xt = [misc((1024,), F32) for _ in range(2)]
    hb = misc((1024,), BF16)
    junk = misc((1024,), BF16)
    Dd = misc((256,), F32)
    Di = misc((256,), I32)

    hT = carve(HT0, (8, NT), BF16)
    mT = carve(MT0, (8, NT), BF16)
    oaT = carve(OA0, (8, NT), BF16)
    obT = [carve(OB0 + h * 4160, (NT,), BF16, 64) for h in range(4)]

    psb = [nc.alloc_psum_tensor("ps%d" % i, [128, 512], F32).ap() for i in range(6)]
    GEN = [0, 1, 2, 3]
    gen_i = [0]

    def psum():
        i = GEN[gen_i[0] % len(GEN)]
        gen_i[0] += 1
        return psb[i], ("ps", i)

    pstb = [nc.alloc_psum_tensor("pst%d" % i, [128, 1024], BF16).ap() for i in range(2)]
    evi = [0]
    GLA_PREFETCH = [True]

    def evac_eng():
        evi[0] += 1
        return "act" if evi[0] % 2 else "dve"

    def copy_op(eng, out, in_, r, w, scale=None):
        if scale is None:
            P.cp(eng, out, in_, r, w)
        elif eng == "act":
            P.op("act", lambda e: e.mul(out, in_, scale), r=r, w=w)
        else:
            P.tsm("dve", out, in_, scale, r, w)

    def tokn(tb):
        return 128 if tb < 16 else NS

    def consts():
        P.memset("pool", ident, 1.0, ["ident"])
        P.asel(ident, ident, [[-1, 128]], ALU.is_equal, 0.0, 0, 1, ["ident"], ["ident"])
        P.memset("pool", ones64, 1.0, ["ones64"])
        P.memset("pool", triI, 1.0, ["triI"])
        P.asel(triI, triI, [[1, 128]], ALU.is_ge, 0.0, 0, -1, ["triI"], ["triI"])
        P.memset("pool", triU, 1.0, ["triU"])
        P.asel(triU, triU, [[-1, 128]], ALU.is_gt, 0.0, 0, 1, ["triU"], ["triU"])
        P.memset("pool", stat, 0.0, ["stat"])
        P.op("pool", lambda e: e.iota(Di[:, 0:128], pattern=[[1, 128]], base=0, channel_multiplier=-1), w=["Di"])
        P.op("pool", lambda e: e.iota(Di[:, 128:256], pattern=[[1, 128]], base=128, channel_multiplier=-1), r=["Di"], w=["Di"])
        P.cp("pool", Dd, Di, ["Di"], ["Dd"])
        P.dma("sp", g1T, norm1_g.rearrange("o (k p) -> p (o k)", p=128), w=["g1T"], allow_slow_non_contiguous=True)
        P.dma("sp", g2T, norm2_g.rearrange("o (k p) -> p (o k)", p=128), w=["g2T"], allow_slow_non_contiguous=True)
        P.dma("sp", gf_bc, normf_g.partition_broadcast(128), w=["gf_bc"])
        P.dma("sp", gn_bc, gng_d.partition_broadcast(128), w=["gn_bc"])
        P.dma("sp", gbias_bc, gb_d.partition_broadcast(128), w=["gbias_bc"])
        P.dma("pool", w2_sb, w2_d, w=["w2"])

    hbufs = [hb, junk]

    def nt_front(src, src_keys, n, tb, statcol):
        sc = stat[0:n, statcol:statcol + 1]
        sk_ = ("stat", statcol)
        hbuf = hbufs[tb % 2]
        hk = ("hb", tb % 2)
        P.act(hbuf[0:n, :], src, AF.Square, src_keys, [hk, sk_], accum_out=sc)
        P.act(sc, sc, AF.Sqrt, [sk_], [sk_], bias=float(D * EPS))
        P.op("dve", lambda e, v_=sc: e.reciprocal(v_, v_), r=[sk_], w=[sk_])
        P.ts("dve", hbuf[0:n, :], src, sc, float(math.sqrt(D)), ALU.mult, ALU.mult, list(src_keys) + [sk_], [hk])

    def nt_back(n, tb, gT, gkey, dst, dst_keys):
        hbuf = hbufs[tb % 2]
        hk = ("hb", tb % 2)
        for half in range(2):
            for j in range(4):
                kc = half * 4 + j
                P.tr(pstb[half][:, j * 128:j * 128 + n], hbuf[0:n, kc * 128:(kc + 1) * 128], ident[0:n, 0:n], [hk, "ident"], [("pst", half)])
            for j in range(4):
                kc = half * 4 + j
                o_ = dst[:, kc, tb * 128:tb * 128 + n]
                i_ = pstb[half][:, j * 128:j * 128 + n]
                g_ = gT[:, kc:kc + 1]
                if half == 0:
                    P.op("act", lambda e, o_=o_, i_=i_, g_=g_: e.mul(o_, i_, g_), r=[("pst", half), gkey], w=dst_keys)
                else:
                    P.tsm("dve", o_, i_, g_, [("pst", half), gkey], dst_keys)

    def phase1():
        for tb in range(NB + 1):
            if tb < NB:
                n = tokn(tb)
                s = tb % 2
                P.dma("sp", xt[s][0:n, :], x_all[tb * 128:tb * 128 + n, :], w=[("xt", s)])
                nt_front(xt[s][0:n, :], [("xt", s)], n, tb, tb)
            if tb >= 1:
                nt_back(tokn(tb - 1), tb - 1, g1T, "g1T", hT, [("hT", tb - 1)])

    HT_ALL = [("hT", tb) for tb in range(NB)]
    NBLK = [(0, 512), (512, 512), (1024, 512), (1536, 512), (2048, NS)]

    def hkeys(c0, n):
        return [("hT", tb) for tb in range(c0 // 128, (c0 + n + 127) // 128)]

    def load_w(dst3, wd, c0, ncols, wkey):
        P.dma("pool", dst3, wd[:, c0:c0 + ncols].rearrange("(k p) n -> p k n", p=128), w=[wkey])

    def proj_fm(wv, wkey, dst, dstkey, scale=None, rows=128):
        for (c0, n) in NBLK:
            ps, pk = psum()
            for kc in range(8):
                P.mm(ps[0:rows, 0:n], wv[:, kc, :], hT[:, kc, c0:c0 + n], kc == 0, kc == 7, [wkey] + hkeys(c0, n), [pk])
            copy_op(evac_eng(), dst[:, c0:c0 + n], ps[0:rows, 0:n], [pk], [dstkey], scale=scale)

    def attention():
        A = Bump(MT0, MT0 + 33 * KB)
        qT = [A((NT,), BF16) for _ in range(3)]
        kT = [A((NT,), BF16) for _ in range(3)]
        stage = [A((256,), F32) for _ in range(2)]
        sbs = [A((256,), F32) for _ in range(4)]
        pp = [A((256,), BF16) for _ in range(4)]
        B2 = Bump(OA0, OA0 + 33 * KB)
        vcls = [B2((16, 128), BF16) for _ in range(3)]
        Uacc = B2((NT,), F32, 64)
        Zacc = B2((NT,), F32, 64)
        C3 = Bump(TMP0, ARENA)
        bias = [[C3((256,), F32) for h in range(2)] for g in range(3)]
        kgrp = [B2((4, 128), BF16) for _ in range(3)]
        vgrp = [C3((4, 128), BF16) for _ in range(3)]
        kTb = [C3((128,), BF16) for _ in range(3)]
        sbias0 = [C3((4,), F32) for _ in range(2)]
        sbiasg = C3((8,), F32)
        nbias = [[C3((4,), F32, 4) for h in range(2)] for g in range(3)]
        stage_s = [C3((256,), F32, 4) for _ in range(2)]
        vnew_t = {(g, sb): C3((128,), BF16, 4) for g in range(3) for sb in range(4)}
        pnew_t = [C3((4,), BF16, 4) for _ in range(8)]
        sbs_s = [C3((4,), F32) for _ in range(3)]
        pp_s2 = [[C3((4,), BF16) for _ in range(2)] for _ in range(3)]
        UaccS = [C3((16,), F32, 64) for _ in range(2)]
        ZaccS = [C3((16,), F32, 64) for _ in range(2)]
        UZ = [psb[4], psb[5]]
        cnt = {"stage": 0, "sbs": 0, "uz": 0, "kb": 0, "ss": 0, "pn": 0, "sbss": 0}

        def cls_cols(g, B):
            r = RATES[g]
            per = 16 // r
            c = B // per
            m0 = (B % per) * 128
            return c + r * m0, r

        def csl(st, rr, n=128):
            return slice(st, st + rr * (n - 1) + 1, rr)

        for hp in range(2):
            for g in range(3):
                for j in range(2):
                    h = 2 * hp + j
                    sl = slope(g, h) * RATES[g]
                    bt = bias[g][j]
                    bk = ("bias", g, j)
                    P.tsm("pool", bt, Dd, -sl, ["Dd"], [bk])
                    P.asel(bt[:, 0:128], bt[:, 0:128], [[1, 128]], ALU.is_ge, NEG, 0, -1, [bk], [bk])
                    P.asel(bt[:, 128:256], bt[:, 128:256], [[-1, 128]], ALU.is_ge, NEG, 0, 1, [bk], [bk])
                    if g == 0:
                        sb0 = sbias0[j]
                        P.tsm("pool", sb0, Dd[:, 128:132], -sl, ["Dd"], [("sbias0", j)])
                        P.asel(sb0, sb0, [[-1, 4]], ALU.is_ge, NEG, 0, 1, [("sbias0", j)], [("sbias0", j)])
                    else:
                        col = (g - 1) * 2 + j
                        P.tsm("pool", sbiasg[:, col:col + 1], Dd[:, 128:129], -sl, ["Dd"], [("sbiasg", col)])
                    nb_ = nbias[g][j]
                    nk = ("nbias", g, j)
                    if g == 0:
                        P.tsm("pool", nb_, Dd[0:4, 0:4], -sl, ["Dd"], [nk])
                        P.asel(nb_, nb_, [[1, 4]], ALU.is_ge, NEG, 0, -1, [nk], [nk])
                    else:
                        P.memset("pool", nb_, 0.0, [nk])
                        P.asel(nb_, nb_, [[1, 4]], ALU.is_equal, NEG, 0, -1, [nk], [nk])
            for g in range(3):
                sl_ap, sk = slot()
                wv = sl_ap[:, 0:8 * 384].rearrange("p (k n) -> p k n", k=8)
                for i, cbase in enumerate((C_DQ, C_DK, C_DV)):
                    load_w(wv[:, :, i * 128:(i + 1) * 128], w_in, cbase + g * 256 + hp * 128, 128, sk)
                proj_fm(wv[:, :, 0:128], sk, qT[g], ("qT", g), scale=0.125)
                proj_fm(wv[:, :, 128:256], sk, kT[g], ("kT", g))
                r = RATES[g]
                per = 16 // r
                for B in range(16):
                    st, rr = cls_cols(g, B)
                    ps, pk = psum()
                    for kc in range(8):
                        P.mm(ps[:, 0:256], hT[:, kc, csl(st, rr)], wv[:, kc, 128:384], kc == 0, kc == 7, [sk] + HT_ALL[0:16], [pk])
                    P.cp("dve", vcls[g][:, B, :], ps[:, 128:256], [pk], [("vcls", g, B)])
                    if g == 0:
                        need = (B == 15)
                    elif g == 1:
                        need = (B % per) == per - 1
                    else:
                        need = True
                    if need:
                        si = cnt["stage"] % 2
                        cnt["stage"] += 1
                        P.cp("act", stage[si], ps[:, 0:256], [pk], [("stage", si)])
                        row0 = st - (T - WINS[g])
                        for (dst, c0) in ((wk_p[g], 0), (wv_p[g], 128)):
                            P.dma("sp", dst[csl(row0, rr), hp * 128:(hp + 1) * 128], stage[si][:, c0:c0 + 128], r=[("stage", si)])
                for sb in range(4):
                    ps, pk = psum()
                    for kc in range(8):
                        P.mm(ps[0:4, 0:256], hT[:, kc, T + 4 * sb:T + 4 * sb + 4], wv[:, kc, 128:384], kc == 0, kc == 7, [sk, ("hT", 16)], [pk])
                    si = cnt["ss"] % 2
                    cnt["ss"] += 1
                    P.cp("act", stage_s[si], ps[0:4, 0:256], [pk], [("stage_s", si)])
                    for (dst, c0) in ((wk_s[g], 0), (wv_s[g], 128)):
                        P.dma("sp", dst[sb, WINS[g] - 4:WINS[g], hp * 128:(hp + 1) * 128], stage_s[si][:, c0:c0 + 128], r=[("stage_s", si)])
                    P.cp("dve", vnew_t[(g, sb)], ps[0:4, 128:256], [pk], [("vnew", g, sb)])

            if hp == 1 and GLA_PREFETCH[0]:
                gla_prefetch()
            for j in range(2):
                h = 2 * hp + j
                pb0 = 64 * j
                blocks = [(g, B) for g in range(3) for B in range(16)]
                SK = 3
                uzsel = {}

                def afront(idx):
                    g, B = blocks[idx]
                    r = RATES[g]
                    per = 16 // r
                    st, rr = cls_cols(g, B)
                    hasprev = (B % per) != 0
                    Wd_ = 256 if hasprev else 128
                    qcols = qT[g][pb0:pb0 + 64, csl(st, rr)]
                    ps, pk = psum()
                    P.mm(ps[:, 0:128], kT[g][pb0:pb0 + 64, csl(st, rr)], qcols, True, True, [("qT", g), ("kT", g)], [pk])
                    if hasprev:
                        stp, _ = cls_cols(g, B - 1)
                        P.mm(ps[:, 128:256], kT[g][pb0:pb0 + 64, csl(stp, rr)], qcols, True, True, [("qT", g), ("kT", g)], [pk])
                    si = idx % 4
                    P.tt("dve", sbs[si][:, 0:Wd_], ps[:, 0:Wd_], bias[g][j][:, 0:Wd_], ALU.add, [pk, ("bias", g, j)], [("sbs", si)])
                    P.act(pp[si][:, 0:Wd_], sbs[si][:, 0:Wd_], AF.Exp, [("sbs", si)], [("pp", si)])

                def aback(idx):
                    g, B = blocks[idx]
                    r = RATES[g]
                    per = 16 // r
                    hasprev = (B % per) != 0
                    si = idx % 4
                    i = B % 2
                    if i == 0:
                        uzsel[idx // 2] = cnt["uz"] % 2
                        cnt["uz"] += 1
                    ui = uzsel[idx // 2]
                    uz = UZ[ui]
                    uzk = ("uz", ui)
                    for isU, c0 in ((True, 0), (False, 256)):
                        lhs = vcls[g][:, B, j * 64:(j + 1) * 64] if isU else ones64
                        lk = [("vcls", g, B)] if isU else ["ones64"]
                        oap = uz[0:64, c0 + i * 128:c0 + (i + 1) * 128]
                        P.mm(oap, lhs, pp[si][:, 0:128], True, not hasprev, lk + [("pp", si)], [uzk])
                        if hasprev:
                            lhs2 = vcls[g][:, B - 1, j * 64:(j + 1) * 64] if isU else ones64
                            lk2 = [("vcls", g, B - 1)] if isU else ["ones64"]
                            P.mm(oap, lhs2, pp[si][:, 128:256], False, True, lk2 + [("pp", si)], [uzk])
                    if i == 1:
                        B0 = B - 1
                        st0, rr = cls_cols(g, B0)
                        for (acc, akey, c0) in ((Uacc, "Uacc", 0), (Zacc, "Zacc", 256)):
                            if r == 16:
                                dv = acc[:, 0:T].rearrange("p (i c) -> p c i", c=16)[:, B0:B0 + 2, :]
                                src = uz[0:64, c0:c0 + 256].rearrange("p (c i) -> p c i", c=2)
                            else:
                                dv = acc[:, csl(st0, rr, 256)]
                                src = uz[0:64, c0:c0 + 256]
                            if g == 0:
                                P.cp("act", dv, src, [uzk], [akey])
                            else:
                                P.tt("dve", dv, dv, src, ALU.add, [uzk, akey], [akey])

                for idx in range(len(blocks) + SK):
                    if idx < len(blocks):
                        afront(idx)
                    if idx - SK >= 0:
                        aback(idx - SK)

                P.op("dve", lambda e: e.reciprocal(Zacc[:, 0:T], Zacc[:, 0:T]), r=["Zacc"], w=["Zacc"])
                P.tt("dve", obT[h][:, 0:T], Uacc[:, 0:T], Zacc[:, 0:T], ALU.mult, ["Uacc", "Zacc"], [("obT", h)])

            sgroups = [(sb, g) for sb in range(4) for g in range(3)]
            stiles = [(sb, g, tq) for (sb, g) in sgroups for tq in range(1 if g == 0 else 4)]
            pn_of = {}

            def issue_grp(n):
                sb, g = sgroups[n]
                gi = n % 3
                r = RATES[g]
                cs = slice(hp * 128, (hp + 1) * 128)
                if g == 0:
                    P.dma("pool", kgrp[gi][:, 0, :], st_k[0][sb, :, cs], w=[("kgrp", gi)])
                    P.dma("pool", vgrp[gi][:, 0, :], st_v[0][sb, :, cs], w=[("vgrp", gi)])
                else:
                    P.dma("pool", kgrp[gi], st_k[g][sb].rearrange("(i t) c -> i t c", t=r)[:, 0:4, cs], w=[("kgrp", gi)])
                    P.dma("pool", vgrp[gi], st_v[g][sb].rearrange("(i t) c -> i t c", t=r)[:, 0:4, cs], w=[("vgrp", gi)])

            def sfa(ti):
                sb, g, tq = stiles[ti]
                n = sgroups.index((sb, g))
                gi = n % 3
                ki = ti % 3
                pb_ = ti % 2
                P.tr(pstb[pb_][:, 0:128], kgrp[gi][:, tq, :], ident, [("kgrp", gi), "ident"], [("pst", pb_)])
                P.cp("dve", kTb[ki], pstb[pb_][:, 0:128], [("pst", pb_)], [("kTb", ki)])

            def sfb(ti):
                sb, g, tq = stiles[ti]
                n = sgroups.index((sb, g))
                q0 = T + 4 * sb
                ki = ti % 3
                if tq == 0:
                    if n + 2 < len(sgroups):
                        issue_grp(n + 2)
                    for j in range(2):
                        pb0 = 64 * j
                        ps, pk = psum()
                        P.mm(ps[0:4, 0:4], kT[g][pb0:pb0 + 64, q0:q0 + 4], qT[g][pb0:pb0 + 64, q0:q0 + 4], True, True, [("qT", g), ("kT", g)], [pk])
                        sn = cnt["sbss"] % 3
                        cnt["sbss"] += 1
                        P.tt("dve", sbs_s[sn][0:4, :], ps[0:4, 0:4], nbias[g][j], ALU.add, [pk, ("nbias", g, j)], [("sbs_s", sn)])
                        pni = cnt["pn"] % 8
                        cnt["pn"] += 1
                        P.act(pnew_t[pni], sbs_s[sn][0:4, :], AF.Exp, [("sbs_s", sn)], [("pnew", pni)])
                        pn_of[(sb, g, j)] = pni
                nq = 4 if g == 0 else 1
                t0 = 0 if g == 0 else tq
                for j in range(2):
                    pb0 = 64 * j
                    ps, pk = psum()
                    P.mm(ps[:, 0:nq], kTb[ki][pb0:pb0 + 64, :], qT[g][pb0:pb0 + 64, q0 + t0:q0 + t0 + nq], True, True, [("kTb", ki), ("qT", g)], [pk])
                    ppk = ("pp_s", ki, j)
                    if g == 0:
                        sn = cnt["sbss"] % 3
                        cnt["sbss"] += 1
                        P.tt("dve", sbs_s[sn], ps[:, 0:4], sbias0[j], ALU.add, [pk, ("sbias0", j)], [("sbs_s", sn)])
                        P.act(pp_s2[ki][j], sbs_s[sn], AF.Exp, [("sbs_s", sn)], [ppk])
                    else:
                        col = (g - 1) * 2 + j
                        P.act(pp_s2[ki][j][:, 0:1], ps[:, 0:1], AF.Exp, [pk, ("sbiasg", col)], [ppk], bias=sbiasg[:, col:col + 1])

            def sback(ti):
                sb, g, tq = stiles[ti]
                n = sgroups.index((sb, g))
                gi = n % 3
                ki = ti % 3
                nq = 4 if g == 0 else 1
                t0 = 0 if g == 0 else tq
                vn = vnew_t[(g, sb)]
                for j in range(2):
                    uz = UZ[j]
                    uzk = ("uz", j)
                    pni = pn_of[(sb, g, j)]
                    pnew = pnew_t[pni]
                    pnk = ("pnew", pni)
                    ppk = ("pp_s", ki, j)
                    for isU, c0 in ((True, 0), (False, 256)):
                        lhs = vgrp[gi][:, tq, j * 64:(j + 1) * 64] if isU else ones64
                        lk = [("vgrp", gi)] if isU else ["ones64"]
                        cc = c0 + g * 4 + t0
                        P.mm(uz[0:64, cc:cc + nq], lhs, pp_s2[ki][j][:, 0:nq], True, False, lk + [ppk], [uzk])
                        lhs2 = vn[:, j * 64:(j + 1) * 64] if isU else ones64[0:4, :]
                        lk2 = [("vnew", g, sb)] if isU else ["ones64"]
                        P.mm(uz[0:64, cc:cc + nq], lhs2, pnew[:, t0:t0 + nq], False, True, lk2 + [pnk], [uzk])
                if ti + 1 == len(stiles) or stiles[ti + 1][0] != sb:
                    for j in range(2):
                        uz = UZ[j]
                        uzk = ("uz", j)
                        for (acc, akey, c0) in ((UaccS[j], ("UaccS", j), 0), (ZaccS[j], ("ZaccS", j), 256)):
                            dv = acc[:, 4 * sb:4 * sb + 4]
                            P.cp("dve", dv, uz[0:64, c0:c0 + 4], [uzk], [akey])
                            for g2 in (1, 2):
                                P.tt("dve", dv, dv, uz[0:64, c0 + 4 * g2:c0 + 4 * g2 + 4], ALU.add, [uzk, akey], [akey])

            issue_grp(0)
            issue_grp(1)
            NTL = len(stiles)
            sfa(0)
            for ti in range(NTL + 1):
                if ti + 1 < NTL:
                    sfa(ti + 1)
                if ti >= 1:
                    sback(ti - 1)
                if ti < NTL:
                    sfb(ti)
            for j in range(2):
                h = 2 * hp + j
                P.op("dve", lambda e, z_=ZaccS[j]: e.reciprocal(z_, z_), r=[("ZaccS", j)], w=[("ZaccS", j)])
                P.tt("dve", obT[h][:, T:T + NS], UaccS[j], ZaccS[j], ALU.mult, [("UaccS", j), ("ZaccS", j)], [("obTs", h)])

        for g in range(3):
            for (src, dst) in ((st_k[g], wk_s[g]), (st_v[g], wv_s[g])):
                for sb in range(4):
                    P.dma("sp", dst[sb, 0:WINS[g] - 4, :], src[sb, 4:WINS[g], :])

    GLA_W = []

    def gla_prefetch():
        for i in range(6):
            s_, k_ = slot()
            v_ = s_[:, 0:4096].rearrange("p (k n) -> p k n", k=8)
            load_w(v_, w_in, i * 512, 512, k_)
            GLA_W.append((v_, k_))

    def gla():
        A = Bump(MT0, MT0 + 33 * KB)
        C3 = Bump(TMP0, ARENA)
        glrT = A((NT,), BF16, 16)
        tri16 = A((128,), F32)
        triU16 = A((128,), F32)
        mask4 = A((4, 128), BF16)
        ktok = [A((512,), BF16) for _ in range(2)]
        vtok = [A((1024,), BF16) for _ in range(2)]
        sgr = [A((1024,), BF16) for _ in range(2)]
        la = [A((512,), F32) for _ in range(2)]
        qTc = A((4, 128), BF16)
        kTc = A((4, 128), BF16)
        eb = A((4, 128), F32)
        enb = A((4, 128), F32)
        ebb = A((512,), F32)
        qe = A((4, 128), BF16)
        ke = A((4, 128), BF16)
        kd = A((512,), BF16)
        AT = A((4, 128), BF16)
        S = [C3((4, 256), F32) for _ in range(2)]
        Sbf = [C3((4, 256), BF16) for _ in range(2)]
        o1 = [C3((256,), F32) for _ in range(2)]
        oat = C3((1024,), BF16)
        jq = C3((256,), BF16)
        gst = C3((8,), F32)
        epsb = C3((1,), F32)
        P.memset("pool", epsb, float(256 * EPS), ["epsb"])
        GEN6 = [0, 1, 2, 3]
        gi = [0]

        def ps6():
            i = GEN6[gi[0] % 4]
            gi[0] += 1
            return psb[i], ("ps", i)

        P.tsm("pool", tri16, triI, -1.0 / 16.0, ["triI"], ["tri16"])
        P.tsm("pool", triU16, triU, -1.0 / 16.0, ["triU"], ["triU16"])
        for hh in range(4):
            P.cp("pool", mask4[:, hh, :], triI, ["triI"], ["mask4"])

        wsl = GLA_W
        wq, wk = wsl[0], wsl[1]
        wvv = [wsl[2], wsl[3]]
        wgr = [wsl[4], wsl[5]]
        s_, k_ = wsl[5]
        wglr = C3((8, 16), BF16)
        load_w(wglr, w_in, C_GLR, 16, "wglr")
        for (c0, n) in NBLK:
            ps, pk = ps6()
            for kc in range(8):
                P.mm(ps[0:16, 0:n], wglr[:, kc, :], hT[:, kc, c0:c0 + n], kc == 0, kc == 7, ["wglr"], [pk])
            P.cp("act", glrT[:, c0:c0 + n], ps[0:16, 0:n], [pk], ["glrT"])

        chunks = [("p", 0, c * 128, 128, c == 0, c == 15) for c in range(16)] + [("s", sb, T + 4 * sb, 4, True, True) for sb in range(4)]

        def f_a(ci):
            kind, sb, c0, C, first, last = chunks[ci]
            b = ci % 2
            ps, pk = ps6()
            for kc in range(8):
                P.mm(ps[0:C, :], hT[:, kc, c0:c0 + C], wk[0][:, kc, :], kc == 0, kc == 7, [wk[1]], [pk])
            P.cp("act", ktok[b][0:C, :], ps[0:C, :], [pk], [("ktok", b)])
            for hf in range(2):
                ps, pk = ps6()
                for kc in range(8):
                    P.mm(ps[0:C, :], hT[:, kc, c0:c0 + C], wvv[hf][0][:, kc, :], kc == 0, kc == 7, [wvv[hf][1]], [pk])
                P.cp("dve", vtok[b][0:C, hf * 512:(hf + 1) * 512], ps[0:C, :], [pk], [("vtok", b)])
        def f_b(ci):
            kind, sb, c0, C, first, last = chunks[ci]
            b = ci % 2
            for hf in range(2):
                ps, pk = ps6()
                for kc in range(8):
                    P.mm(ps[0:C, :], hT[:, kc, c0:c0 + C], wgr[hf][0][:, kc, :], kc == 0, kc == 7, [wgr[hf][1]], [pk])
                P.act(sgr[b][0:C, hf * 512:(hf + 1) * 512], ps[0:C, :], AF.Silu, [pk], [("sgr", b)])
            ps, pk = ps6()
            P.mm(ps[0:C, :], glrT[:, c0:c0 + C], w2_sb, True, True, ["glrT", "w2"], [pk])
            P.tt("dve", la[b][0:C, :], ps[0:C, :], gbias_bc[0:C, :], ALU.add, [pk, "gbias_bc"], [("la", b)])
            P.act(la[b][0:C, :], la[b][0:C, :], AF.Exp, [("la", b)], [("la", b)], scale=-1.0)
            P.act(la[b][0:C, :], la[b][0:C, :], AF.Ln, [("la", b)], [("la", b)], bias=1.0)

        def m_a(ci):
            kind, sb, c0, C, first, last = chunks[ci]
            b = ci % 2
            psq, pkq = ps6()
            for h in range(4):
                for kc in range(8):
                    P.mm(psq[:, h * 128:h * 128 + C], wq[0][:, kc, h * 128:(h + 1) * 128], hT[:, kc, c0:c0 + C], kc == 0, kc == 7, [wq[1]], [pkq])
            P.tsm("dve", qTc[:, :, 0:C], psq.rearrange("p (h t) -> p h t", h=4)[:, :, 0:C], 128.0 ** -0.5, [pkq], ["qTc"])
            psk, pkk = ps6()
            for h in range(4):
                for kc in range(8):
                    P.mm(psk[:, h * 128:h * 128 + C], wk[0][:, kc, h * 128:(h + 1) * 128], hT[:, kc, c0:c0 + C], kc == 0, kc == 7, [wk[1]], [pkk])
            P.cp("act", kTc[:, :, 0:C], psk.rearrange("p (h t) -> p h t", h=4)[:, :, 0:C], [pkk], ["kTc"])
        def m_b(ci):
            kind, sb, c0, C, first, last = chunks[ci]
            b = ci % 2
            psb1, pk1 = ps6()
            for h in range(4):
                P.mm(psb1[:, h * 128:h * 128 + C], la[b][0:C, h * 128:(h + 1) * 128], tri16[0:C, 0:C], True, True, [("la", b), "tri16"], [pk1])
            psb2, pk2 = ps6()
            P.mm(psb2[0:C, :], triU16[0:C, 0:C], la[b][0:C, :], True, True, [("la", b), "triU16"], [pk2])
            bv = psb1.rearrange("p (h t) -> p h t", h=4)[:, :, 0:C]
            P.act(eb[:, :, 0:C], bv, AF.Exp, [pk1], ["eb"])
            P.act(enb[:, :, 0:C], bv, AF.Exp, [pk1], ["enb"], scale=-1.0)
            P.act(ebb[0:C, :], psb2[0:C, :], AF.Exp, [pk2], ["ebb"])
            P.tt("dve", qe[:, :, 0:C], qTc[:, :, 0:C], eb[:, :, 0:C], ALU.mult, ["qTc", "eb"], ["qe"])
            P.tt("dve", ke[:, :, 0:C], kTc[:, :, 0:C], enb[:, :, 0:C], ALU.mult, ["kTc", "enb"], ["ke"])
            P.tt("dve", kd[0:C, :], ktok[b][0:C, :], ebb[0:C, :], ALU.mult, [("ktok", b), "ebb"], ["kd"])
        def m_c(ci):
            kind, sb, c0, C, first, last = chunks[ci]
            b = ci % 2
            psa, pka = ps6()
            for h in range(4):
                P.mm(psa[0:C, h * 128:h * 128 + C], ke[:, h, 0:C], qe[:, h, 0:C], True, True, ["ke", "qe"], [pka])
            P.tt("dve", AT[0:C, :, 0:C], psa.rearrange("p (h t) -> p h t", h=4)[0:C, :, 0:C], mask4[0:C, :, 0:C], ALU.mult, [pka, "mask4"], ["AT"])

        pso_of = {}

        def b_a(ci):
            kind, sb, c0, C, first, last = chunks[ci]
            b = ci % 2
            sidx = 0 if kind == "p" else 1
            St, Sb = S[sidx], Sbf[sidx]
            Sk, Sbk = ("S", sidx), ("Sbf", sidx)
            if first:
                if kind == "p":
                    P.memset("pool", St, 0.0, [Sk])
                    P.memset("pool", Sb, 0.0, [Sbk])
                else:
                    P.dma("sp", St, st_gla[sb].rearrange("h k v -> k h v"), w=[Sk])
                    P.cp("act", Sb, St, [Sk], [Sbk])
            pso = [(psb[4], ("ps", 4)), (psb[5], ("ps", 5))]
            pso_of[ci] = pso
            for h in range(4):
                o_ap = pso[h // 2][0][0:C, (h % 2) * 256:(h % 2) * 256 + 256]
                ok = pso[h // 2][1]
                P.mm(o_ap, AT[0:C, h, 0:C], vtok[b][0:C, h * 256:(h + 1) * 256], True, False, ["AT", ("vtok", b)], [ok])
                P.mm(o_ap, qe[:, h, 0:C], Sb[:, h, :], False, True, ["qe", Sbk], [ok])
            psd = [ps6(), ps6()]
            for h in range(4):
                d_ap = psd[h // 2][0][:, (h % 2) * 256:(h % 2) * 256 + 256]
                P.mm(d_ap, kd[0:C, h * 128:(h + 1) * 128], vtok[b][0:C, h * 256:(h + 1) * 256], True, True, ["kd", ("vtok", b)], [psd[h // 2][1]])
            for h in range(4):
                d_ap = psd[h // 2][0][:, (h % 2) * 256:(h % 2) * 256 + 256]
                P.stt("dve", St[:, h, :], St[:, h, :], eb[:, h, C - 1:C], d_ap, ALU.mult, ALU.add, [Sk, "eb", psd[h // 2][1]], [Sk])
            P.cp("act", Sb, St, [Sk], [Sbk])
        def b_b(ci):
            kind, sb, c0, C, first, last = chunks[ci]
            b = ci % 2
            pso = pso_of[ci]
            gs = (ci % 2) * 4
            gk = ("gst", ci % 2)
            P.memset("pool", gst[0:C, gs:gs + 4], 0.0, [gk])
            for h in range(4):
                o_ap = pso[h // 2][0][0:C, (h % 2) * 256:(h % 2) * 256 + 256]
                P.act(jq[0:C, :], o_ap, AF.Square, [pso[h // 2][1]], ["jq", gk], accum_out=gst[0:C, gs + h:gs + h + 1])
            gv = gst[0:C, gs:gs + 4]
            P.act(gv, gv, AF.Ln, [gk, "epsb"], [gk], bias=epsb[0:C, 0:1])
            P.act(gv, gv, AF.Exp, [gk], [gk], scale=-0.5)
            for h in range(4):
                o_ap = pso[h // 2][0][0:C, (h % 2) * 256:(h % 2) * 256 + 256]
                ob_ = o1[h % 2]
                P.stt("dve", ob_[0:C, :], o_ap, gst[0:C, gs + h:gs + h + 1], gn_bc[0:C, :], ALU.mult, ALU.mult,
                      [pso[h // 2][1], gk, "gn_bc"], [("o1", h % 2)])
                P.stt("dve", oat[0:C, h * 256:(h + 1) * 256], ob_[0:C, :], 16.0, sgr[b][0:C, h * 256:(h + 1) * 256], ALU.mult, ALU.mult,
                      [("o1", h % 2), ("sgr", b)], ["oat"])
        def b_c(ci):
            kind, sb, c0, C, first, last = chunks[ci]
            b = ci % 2
            sidx = 0 if kind == "p" else 1
            St = S[sidx]
            Sk = ("S", sidx)
            pb_ = ci % 2
            for jj in range(8):
                P.tr(pstb[pb_][:, jj * 128:jj * 128 + C], oat[0:C, jj * 128:(jj + 1) * 128], ident[0:C, 0:C], ["oat", "ident"], [("pst", pb_)])
            pv = pstb[pb_].rearrange("p (a b) -> p a b", a=8)[:, :, 0:C]
            P.cp("act", oaT[:, :, c0:c0 + C], pv, [("pst", pb_)], [("oaT", c0)])
            if last:
                dst = gla_p if kind == "p" else gla_s[sb]
                P.dma("sp", dst.rearrange("h k v -> k h v"), St, r=[Sk])

        n = len(chunks)
        f_a(0)
        f_b(0)
        m_a(0)
        for ci in range(n):
            m_b(ci)
            if ci >= 1:
                b_c(ci - 1)
            if ci + 1 < n:
                f_a(ci + 1)
            m_c(ci)
            if ci + 1 < n:
                f_b(ci + 1)
            b_a(ci)
            if ci + 1 < n:
                m_a(ci + 1)
            b_b(ci)
        b_c(n - 1)

    X1_0 = HT0
    x1 = carve(X1_0, (NB, 1024), F32)
    AT0 = (X1_0 + NB * 4096 + 63) // 32 * 32
    assert AT0 + 30 * KB <= ARENA

    def phase3a():
        GEN[:] = [0, 1, 2, 3, 4, 5]
        C3 = Bump(TMP0, ARENA)
        sga = [C3((512,), F32) for _ in range(2)]
        sgb = [C3((512,), F32) for _ in range(2)]
        spk = "pbv"
        pbv = C3((4, 1024), BF16, 64)
        for hh in range(4):
            P.dma("pool", pbv[:, hh, :], proj_b[hh * 64:(hh + 1) * 64, :], w=[spk])
        k = 0
        import os
        V3 = os.environ.get('V3', 'gpbm')
        for oc in range(int(os.environ.get('NOC', '8'))):
            sl_ap, sk = slot()
            wv = sl_ap[:, 0:8 * 384].rearrange("p (k n) -> p k n", k=8)
            load_w(wv[:, :, 0:128], w_in, C_GA + oc * 128, 128, sk)
            load_w(wv[:, :, 128:256], w_in, C_GB + oc * 128, 128, sk)
            load_w(wv[:, :, 256:384], proj_a, oc * 128, 128, sk)
            for (c0, n) in NBLK:
                b = k % 2
                k += 1
                mk = [("mT", tb) for tb in range(c0 // 128, (c0 + n + 127) // 128)]
                if 'g' in V3:
                    pga, kga = psum()
                    for kc in range(8):
                        P.mm(pga[:, 0:n], wv[:, kc, 0:128], hT[:, kc, c0:c0 + n], kc == 0, kc == 7, [sk], [kga])
                    P.act(sga[b][:, 0:n], pga[:, 0:n], AF.Sigmoid, [kga], [("sga", b)])
                    pgb, kgb = psum()
                    for kc in range(8):
                        P.mm(pgb[:, 0:n], wv[:, kc, 128:256], hT[:, kc, c0:c0 + n], kc == 0, kc == 7, [sk], [kgb])
                    P.act(sgb[b][:, 0:n], pgb[:, 0:n], AF.Sigmoid, [kgb], [("sgb", b)])
                if 'p' in V3:
                    ppa, kpa = psum()
                    for kc in range(8):
                        P.mm(ppa[:, 0:n], wv[:, kc, 256:384], oaT[:, kc, c0:c0 + n], kc == 0, kc == 7, [sk], [kpa])
                    P.tt("dve", sga[b][:, 0:n], sga[b][:, 0:n], ppa[:, 0:n], ALU.mult, [kpa, ("sga", b)], [("sga", b)])
                if 'b' in V3:
                    ppb, kpb = psum()
                    for hh in range(4):
                        P.mm(ppb[:, 0:n], pbv[:, hh, oc * 128:(oc + 1) * 128], obT[hh][:, c0:c0 + n], hh == 0, hh == 3, [spk], [kpb])
                    P.tt("dve", sgb[b][:, 0:n], sgb[b][:, 0:n], ppb[:, 0:n], ALU.mult, [kpb, ("sgb", b)], [("sgb", b)])
                if 'm' in V3:
                    P.tt("dve", mT[:, oc, c0:c0 + n], sga[b][:, 0:n], sgb[b][:, 0:n], ALU.add, [("sga", b), ("sgb", b)], mk)

    WO = []

    def p3b_prefetch():
        for hf in range(2):
            s_, k_ = slot()
            v = s_[:, 0:4096].rearrange("p (k n) -> p k n", k=8)
            load_w(v, w_out, hf * 512, 512, k_)
            WO.append((v, k_))

    def phase3b():
        wo = WO
        for tb in range(NB + 1):
            if tb < NB:
                n = tokn(tb)
                s = tb % 2
                P.dma("sp", xt[s][0:n, :], x_all[tb * 128:tb * 128 + n, :], w=[("xt", s)])
                for hf in range(2):
                    ps, pk = psum()
                    v, k_ = wo[hf]
                    for kc in range(8):
                        P.mm(ps[0:n, :], mT[:, kc, tb * 128:tb * 128 + n], v[:, kc, :], kc == 0, kc == 7, [k_, ("mT", tb)], [pk])
                    P.tt("dve", x1[0:n, tb, hf * 512:(hf + 1) * 512], ps[0:n, :], xt[s][0:n, hf * 512:(hf + 1) * 512], ALU.add,
                         [pk, ("xt", s)], [("x1", tb)])
                nt_front(x1[0:n, tb, :], [("x1", tb)], n, tb, 20 + tb)
            if tb >= 1:
                nt_back(tokn(tb - 1), tb - 1, g2T, "g2T", mT, [("mT", tb - 1)])

    P4W = {}

    def p4_load(grp):
        sl1, k1 = slot()
        wgu = sl1[:, 0:4096].rearrange("p (k n) -> p k n", k=8)
        load_w(wgu[:, :, 0:256], wg_d, grp * 256, 256, k1)
        load_w(wgu[:, :, 256:512], wu_d, grp * 256, 256, k1)
        sl2, k2 = slot()
        wdv = sl2[:, 0:2048].rearrange("p (c n) -> p c n", c=2)
        P.dma("pool", wdv, wd_d[grp * 256:(grp + 1) * 256, :].rearrange("(c p) n -> p c n", p=128), w=[k2])
        P4W[grp] = (wgu, k1, wdv, k2)

    def phase4():
        A = Bump(AT0, ARENA)
        aT = [A((2, NT), BF16) for _ in range(2)]
        sgt = [A((512,), F32) for _ in range(2)]
        yt = [A((1024,), F32) for _ in range(2)]
        h2T = mT
        P.tsm("dve", gf_bc, gf_bc, float(math.sqrt(D)), ["gf_bc"], ["gf_bc"])

        def final_norm(tb):
            n = tokn(tb)
            b = tb % 2
            col = 40 + tb
            sc = stat[0:n, col:col + 1]
            sk_ = ("stat", col)
            P.act(junk[0:n, :], x1[0:n, tb, :], AF.Square, [("x1", tb)], ["junk", sk_], accum_out=sc)
            P.act(sc, sc, AF.Sqrt, [sk_], [sk_], bias=float(D * EPS))
            P.op("dve", lambda e, v_=sc: e.reciprocal(v_, v_), r=[sk_], w=[sk_])
            P.stt("dve", yt[b][0:n, :], x1[0:n, tb, :], sc, gf_bc[0:n, :], ALU.mult, ALU.mult, [("x1", tb), sk_, "gf_bc"], [("yt", b)])
            P.dma("sp", y_all[tb * 128:tb * 128 + n, :], yt[b][0:n, :], r=[("yt", b)])

        k = 0
        for grp in range(11):
            ab = grp % 2
            if grp not in P4W:
                p4_load(grp)
            wgu, k1, wdv, k2 = P4W[grp]
            for c in range(2):
                for (c0, n) in NBLK:
                    b = k % 2
                    k += 1
                    pg, kg = psum()
                    for kc in range(8):
                        P.mm(pg[:, 0:n], wgu[:, kc, c * 128:(c + 1) * 128], h2T[:, kc, c0:c0 + n], kc == 0, kc == 7, [k1], [kg])
                    P.act(sgt[b][:, 0:n], pg[:, 0:n], AF.Silu, [kg], [("sgt", b)])
                    pu, ku = psum()
                    for kc in range(8):
                        P.mm(pu[:, 0:n], wgu[:, kc, 256 + c * 128:256 + (c + 1) * 128], h2T[:, kc, c0:c0 + n], kc == 0, kc == 7, [k1], [ku])
                    P.tt("dve", aT[ab][:, c, c0:c0 + n], sgt[b][:, 0:n], pu[:, 0:n], ALU.mult, [ku, ("sgt", b)], [("aT", ab)])
            for tb in range(NB):
                n = tokn(tb)
                for hf in range(2):
                    ps, pk = psum()
                    for c in range(2):
                        P.mm(ps[0:n, :], aT[ab][:, c, tb * 128:tb * 128 + n], wdv[:, c, hf * 512:(hf + 1) * 512], c == 0, c == 1, [k2, ("aT", ab)], [pk])
                    xv = x1[0:n, tb, hf * 512:(hf + 1) * 512]
                    P.tt("dve", xv, xv, ps[0:n, :], ALU.add, [pk, ("x1", tb)], [("x1", tb)])
                if grp == 10 and tb >= 1:
                    final_norm(tb - 1)
            if grp == 10:
                final_norm(NB - 1)

    consts()
    import os
    SKIP = os.environ.get('SKIP', '')
    if stop_after >= 1 and '1' not in SKIP:
        phase1()
    if stop_after >= 2 and '2' not in SKIP:
        attention()
    if stop_after >= 3 and '3' not in SKIP:
        P.barrier()
        gla()
    if stop_after >= 4:
        P.barrier()
        import os
        phase3a()
        if os.environ.get('PH3', 'ab') == 'ab':
            p3b_prefetch()
            P.barrier()
            phase3b()
            if stop_after >= 5:
                p4_load(0)
                p4_load(1)
    if stop_after >= 5:
        P.barrier()
        phase4()
    if debug:
        dbg = {}
        P.barrier()
        for name, ap, shape in (("d_hT", hT, (128, 8 * NT)), ("d_oaT", oaT, (128, 8 * NT)), ("d_mT", mT, (128, 8 * NT))):
            d = nc.dram_tensor(name, [128, 8 * NT], BF16, kind="ExternalOutput").ap()
            P.op("sp", lambda e, d=d, ap=ap: e.dma_start(out=d.rearrange("p (a b) -> p a b", a=8), in_=ap), dma=True)
        for h in range(4):
            d = nc.dram_tensor("d_obT%d" % h, [64, NT], BF16, kind="ExternalOutput").ap()
            P.op("sp", lambda e, d=d, h=h: e.dma_start(out=d, in_=obT[h]), dma=True)
    P.emit()
    return nc


_NC_CACHE = {}


def make_in_maps(inp):
    f = lambda a: np.ascontiguousarray(np.asarray(a, dtype=np.float32))
    xp, xs = f(inp["x_prompt"]), f(inp["x_sample"])
    shared = {
        "norm1_g": f(inp["norm1_g"]).reshape(1, D), "w_in": f(inp["w_in"])[0], "w2": f(inp["gla_gate_w2"])[0],
        "gate_b": f(inp["gla_gate_b"]).reshape(1, 512), "gla_norm_g": f(inp["gla_norm_g"]).reshape(1, 256),
        "proj_a": f(inp["proj_a"])[0], "proj_b": f(inp["proj_b"])[0], "w_out": f(inp["w_out"])[0],
        "norm2_g": f(inp["norm2_g"]).reshape(1, D), "wg": f(inp["w_ffn_gate"])[0], "wu": f(inp["w_ffn_up"])[0],
        "wd": f(inp["w_ffn_down"])[0], "normf_g": f(inp["norm_f_g"]).reshape(1, D),
    }
    sg = f(inp["state_gla"])[0]
    sk = [f(inp["state_win0_k"])[0], f(inp["state_win1_k"])[0], f(inp["state_win2_k"])[0]]
    sv = [f(inp["state_win0_v"])[0], f(inp["state_win1_v"])[0], f(inp["state_win2_v"])[0]]
    maps = []
    for c in range(NCORES):
        m = dict(shared)
        m["x_all"] = np.ascontiguousarray(np.concatenate([xp[c], xs[4 * c:4 * c + 4].reshape(NS, D)], axis=0))
        m["st_gla"] = np.ascontiguousarray(sg[4 * c:4 * c + 4])
        for g in range(3):
            m["st_k%d" % g] = np.ascontiguousarray(sk[g][4 * c:4 * c + 4].reshape(4, WINS[g], 256))
            m["st_v%d" % g] = np.ascontiguousarray(sv[g][4 * c:4 * c + 4].reshape(4, WINS[g], 256))
        maps.append(m)
    return maps


def kernel(**inputs):
    if "nc" not in _NC_CACHE:
        _NC_CACHE["nc"] = build_nc()
    nc = _NC_CACHE["nc"]
    maps = make_in_maps(inputs)
    res = run_bass_kernel_spmd(nc, maps, core_ids=list(range(NCORES)))
    R = res.results
    y_all = np.stack([R[c]["y_all"] for c in range(NCORES)])
    y_prompt = np.ascontiguousarray(y_all[:, :T, :])
    y_sample = np.ascontiguousarray(y_all[:, T:, :].reshape(32, 4, D))
    gla_p = np.stack([R[c]["gla_p"] for c in range(NCORES)])[None]
    outs = [y_prompt, y_sample, gla_p]
    for g in range(3):
        outs.append(np.stack([R[c]["wk_p%d" % g] for c in range(NCORES)]).reshape(1, 8, WINS[g], 4, 64))
        outs.append(np.stack([R[c]["wv_p%d" % g] for c in range(NCORES)]).reshape(1, 8, WINS[g], 4, 64))
    outs.append(np.concatenate([R[c]["gla_s"] for c in range(NCORES)], axis=0)[None])
    for g in range(3):
        outs.append(np.concatenate([R[c]["wk_s%d" % g] for c in range(NCORES)], axis=0).reshape(1, 32, WINS[g], 4, 64))
        outs.append(np.concatenate([R[c]["wv_s%d" % g] for c in range(NCORES)], axis=0).reshape(1, 32, WINS[g], 4, 64))
    return tuple(np.ascontiguousarray(o.astype(np.float32)) for o in outs)
```
